# Optimizing a Trainium2 kernel written in Bass

```python
import jax, jax.numpy as jnp
from jax import lax
import numpy as np

D_MODEL = 2048
BATCH = 16
SEQ = 256
DEPTH = 1
DEC_BATCH = 2
DEC_SEQ = 1024
PAST_LEN = 512

GRID_W = 64
HEAD_DIM = 128
ATTN_WIDTH = D_MODEL // 2
NA_HEADS = ATTN_WIDTH // HEAD_DIM
POOL_WIDTH = D_MODEL - ATTN_WIDTH
POOL_GROUPS = 4
POOL_GROUP_WIDTH = POOL_WIDTH // POOL_GROUPS
POOL_WINDOWS = (2, 4, 8, 16)
MIX_WIDTH = ATTN_WIDTH + POOL_WIDTH
IN_WIDTH = 3 * ATTN_WIDTH + POOL_WIDTH
WIN_R_MAX = 8
WIN_C = 16
STRIP_C = 2 * WIN_C
Q_BLOCK = 128
D_FF = ((8 * D_MODEL + 3 * 256 - 1) // (3 * 256)) * 256
EPS = 1e-6
NEG = -1e30

kernel_name = "hybrid_natten_poolmix_flow_step"


def rms_norm(x, g):
    xf = x.astype(jnp.float32)
    y = xf * lax.rsqrt(jnp.mean(xf * xf, axis=-1, keepdims=True) + EPS)
    return (y * g.astype(jnp.float32)).astype(x.dtype)


def adaln(cond, w_ada, b_ada):
    m = jnp.einsum('bd,de->be', jax.nn.silu(cond), w_ada) + b_ada
    return jnp.split(m[:, None, :], 6, axis=-1)


def modulate(h, shift, scale):
    return h * (1.0 + scale) + shift


def mixer_inputs(h, w_in, q_g, k_g):
    B, L, _ = h.shape
    proj = jnp.einsum('bld,de->ble', h, w_in)
    q, k, v, u = jnp.split(proj, [ATTN_WIDTH, 2 * ATTN_WIDTH, 3 * ATTN_WIDTH], axis=-1)
    q = rms_norm(q.reshape(B, L, NA_HEADS, HEAD_DIM), q_g)
    k = rms_norm(k.reshape(B, L, NA_HEADS, HEAD_DIM), k_g)
    v = v.reshape(B, L, NA_HEADS, HEAD_DIM)
    return q, k, v, u


def context_self_attention(q, k, v):
    B, L, H, Dh = q.shape
    nb = L // Q_BLOCK
    qb = q.reshape(B, nb, Q_BLOCK, H, Dh).transpose(1, 0, 2, 3, 4)

    def one(qblk):
        s = jnp.einsum('bqhd,bkhd->bhqk', qblk, k).astype(jnp.float32) * (Dh ** -0.5)
        p = jax.nn.softmax(s, axis=-1).astype(v.dtype)
        return jnp.einsum('bhqk,bkhd->bqhd', p, v)

    o = lax.map(one, qb)
    return o.transpose(1, 0, 2, 3, 4).reshape(B, L, H * Dh)


def neighbourhood_attention(q, k, v, ck, cv, rpb):
    B, L, H, Dh = q.shape
    rows = L // GRID_W
    win_r = min(WIN_R_MAX, rows)
    n_cb = GRID_W // WIN_C
    nblk = rows * n_cb
    kwin = win_r * STRIP_C
    r = np.arange(rows)
    row_start = np.clip(r - win_r // 2, 0, rows - win_r)
    key_rows = row_start[:, None] + np.arange(win_r)[None, :]
    cb = np.arange(n_cb)
    strip_start = np.clip(cb * WIN_C - WIN_C // 2, 0, GRID_W - STRIP_C)
    key_cols = strip_start[:, None] + np.arange(STRIP_C)[None, :]
    key_idx = (key_rows[:, None, :, None] * GRID_W + key_cols[None, :, None, :]).reshape(nblk, kwin)
    q_cols = cb[:, None] * WIN_C + np.arange(WIN_C)[None, :]
    q_col_start = np.clip(q_cols - WIN_C // 2, 0, GRID_W - WIN_C)
    kc = key_cols[:, None, :]
    col_ok = (kc >= q_col_start[..., None]) & (kc < q_col_start[..., None] + WIN_C)
    mask = np.broadcast_to(col_ok[None, :, :, None, :], (rows, n_cb, WIN_C, win_r, STRIP_C))
    dr = (key_rows - r[:, None]) + (WIN_R_MAX - 1)
    dc = np.clip(key_cols[:, None, :] - q_cols[:, :, None] + (WIN_C - 1), 0, 2 * WIN_C - 2)
    bias = rpb[:, dr[:, None, None, :, None], dc[None, :, :, None, :]].astype(jnp.float32)
    bias = jnp.where(jnp.asarray(mask)[None], bias, NEG).reshape(H, nblk, WIN_C, kwin)

    qb = q.reshape(B, nblk, WIN_C, H, Dh)
    idx = jnp.asarray(key_idx)
    kg = jnp.take(k, idx, axis=1)
    vg = jnp.take(v, idx, axis=1)
    scale = Dh ** -0.5
    s_loc = jnp.einsum('bnqhd,bnkhd->bhnqk', qb, kg).astype(jnp.float32) * scale + bias[None]
    s_ctx = jnp.einsum('bnqhd,bchd->bhnqc', qb, ck).astype(jnp.float32) * scale
    p = jax.nn.softmax(jnp.concatenate([s_loc, s_ctx], axis=-1), axis=-1).astype(v.dtype)
    p_loc, p_ctx = p[..., :kwin], p[..., kwin:]
    o = jnp.einsum('bhnqk,bnkhd->bnqhd', p_loc, vg) + jnp.einsum('bhnqc,bchd->bnqhd', p_ctx, cv)
    return o.reshape(B, L, H * Dh)


def multiscale_pool(u, w_pool, pool_scale):
    B, L, _ = u.shape
    ug = u.reshape(B, L, POOL_GROUPS, POOL_GROUP_WIDTH).astype(jnp.float32)
    cs = jnp.concatenate([jnp.zeros((B, 1, POOL_GROUPS, POOL_GROUP_WIDTH), jnp.float32),
                          jnp.cumsum(ug, axis=1)], axis=1)
    t = np.arange(L)
    means = []
    for g, w in enumerate(POOL_WINDOWS):
        lo = np.clip(t - w // 2, 0, L)
        hi = np.clip(t + w // 2, 0, L)
        cnt = jnp.asarray((hi - lo).astype(np.float32))[None, :, None]
        means.append((cs[:, hi, g] - cs[:, lo, g]) / cnt)
    pooled = jnp.stack(means, axis=2)
    d = (pooled - ug).astype(u.dtype)
    y = jnp.einsum('blgc,gce->blge', d, w_pool)
    return y.reshape(B, L, POOL_WIDTH) * pool_scale


def merge_project(attn, pool, oa_g, op_g, w_out):
    y = jnp.concatenate([rms_norm(attn, oa_g), rms_norm(pool, op_g)], axis=-1)
    return jnp.einsum('ble,ed->bld', y, w_out)


def swiglu(h, w_gate, w_up, w_down):
    a = jax.nn.silu(jnp.einsum('bld,df->blf', h, w_gate)) * jnp.einsum('bld,df->blf', h, w_up)
    return jnp.einsum('blf,fd->bld', a, w_down)


def setup_inputs(seed: int = 0) -> dict:
    key = jax.random.key(seed)
    ks = jax.random.split(key, 24)
    f32 = jnp.float32

    def nrm(k, shape, scale):
        return jax.random.normal(k, shape, f32) * scale

    def gain(k, shape):
        return 1.0 + 0.05 * jax.random.normal(k, shape, f32)

    return {
        "x_prompt": nrm(ks[0], (BATCH, SEQ, D_MODEL), 1.0),
        "x_sample": nrm(ks[1], (DEC_BATCH, DEC_SEQ, D_MODEL), 1.0),
        "cache_k": nrm(ks[2], (DEC_BATCH, DEPTH, PAST_LEN, NA_HEADS, HEAD_DIM), 1.0),
        "cache_v": nrm(ks[3], (DEC_BATCH, DEPTH, PAST_LEN, NA_HEADS, HEAD_DIM), 1.0),
        "c": nrm(ks[4], (DEC_BATCH, D_MODEL), 1.0),
        "c_ctx": nrm(ks[5], (D_MODEL,), 1.0),
        "w_ada": nrm(ks[6], (DEPTH, D_MODEL, 6 * D_MODEL), 0.5 * D_MODEL ** -0.5),
        "b_ada": nrm(ks[7], (DEPTH, 6 * D_MODEL), 0.02),
        "norm1_g": gain(ks[8], (DEPTH, D_MODEL)),
        "w_in": nrm(ks[9], (DEPTH, D_MODEL, IN_WIDTH), D_MODEL ** -0.5),
        "q_norm_g": gain(ks[10], (DEPTH, HEAD_DIM)),
        "k_norm_g": gain(ks[11], (DEPTH, HEAD_DIM)),
        "rpb": nrm(ks[12], (DEPTH, NA_HEADS, 2 * WIN_R_MAX - 1, 2 * WIN_C - 1), 0.1),
        "w_pool": nrm(ks[13], (DEPTH, POOL_GROUPS, POOL_GROUP_WIDTH, POOL_GROUP_WIDTH), POOL_GROUP_WIDTH ** -0.5),
        "pool_scale": gain(ks[14], (DEPTH, POOL_WIDTH)),
        "out_norm_attn_g": gain(ks[15], (DEPTH, ATTN_WIDTH)),
        "out_norm_pool_g": gain(ks[16], (DEPTH, POOL_WIDTH)),
        "w_out": nrm(ks[17], (DEPTH, MIX_WIDTH, D_MODEL), MIX_WIDTH ** -0.5),
        "norm2_g": gain(ks[18], (DEPTH, D_MODEL)),
        "w_gate": nrm(ks[19], (DEPTH, D_MODEL, D_FF), D_MODEL ** -0.5),
        "w_up": nrm(ks[20], (DEPTH, D_MODEL, D_FF), D_MODEL ** -0.5),
        "w_down": nrm(ks[21], (DEPTH, D_FF, D_MODEL), D_FF ** -0.5),
    }


def reference(x_prompt, x_sample, cache_k, cache_v, c, c_ctx, w_ada, b_ada, norm1_g, w_in,
              q_norm_g, k_norm_g, rpb, w_pool, pool_scale, out_norm_attn_g, out_norm_pool_g,
              w_out, norm2_g, w_gate, w_up, w_down):
    x = x_prompt
    new_k, new_v = [], []
    for l in range(DEPTH):
        sh1, sc1, g1, sh2, sc2, g2 = adaln(c_ctx[None, :], w_ada[l], b_ada[l])
        h = modulate(rms_norm(x, norm1_g[l]), sh1, sc1)
        q, k, v, u = mixer_inputs(h, w_in[l], q_norm_g[l], k_norm_g[l])
        attn = context_self_attention(q, k, v)
        pool = multiscale_pool(u, w_pool[l], pool_scale[l])
        x = x + g1 * merge_project(attn, pool, out_norm_attn_g[l], out_norm_pool_g[l], w_out[l])
        h = modulate(rms_norm(x, norm2_g[l]), sh2, sc2)
        x = x + g2 * swiglu(h, w_gate[l], w_up[l], w_down[l])
        new_k.append(k)
        new_v.append(v)
    y_prompt = x
    new_cache_k = jnp.stack(new_k, axis=1)
    new_cache_v = jnp.stack(new_v, axis=1)

    x = x_sample
    for l in range(DEPTH):
        sh1, sc1, g1, sh2, sc2, g2 = adaln(c, w_ada[l], b_ada[l])
        h = modulate(rms_norm(x, norm1_g[l]), sh1, sc1)
        q, k, v, u = mixer_inputs(h, w_in[l], q_norm_g[l], k_norm_g[l])
        attn = neighbourhood_attention(q, k, v, cache_k[:, l], cache_v[:, l], rpb[l])
        pool = multiscale_pool(u, w_pool[l], pool_scale[l])
        x = x + g1 * merge_project(attn, pool, out_norm_attn_g[l], out_norm_pool_g[l], w_out[l])
        h = modulate(rms_norm(x, norm2_g[l]), sh2, sc2)
        x = x + g2 * swiglu(h, w_gate[l], w_up[l], w_down[l])
    y_sample = x

    return (y_prompt, y_sample, new_cache_k, new_cache_v)
```

```python
import numpy as np
import ml_dtypes
import concourse.bass as bass
import concourse.mybir as mybir
from concourse.bass_utils import run_bass_kernel_spmd

F32 = mybir.dt.float32
BF16 = mybir.dt.bfloat16
AF = mybir.ActivationFunctionType
ALU = mybir.AluOpType
AX = mybir.AxisListType

NCORES = 8
D = 2048
NCH = 16
DFF = 5632
NF = 44
EPS = 1e-6
SCALE = 128 ** -0.5
NEG = -1e30
KB = 1024
ARENA_BASE = 16512


class Op:
    __slots__ = ("eng", "kind", "fn", "deps", "signal", "ev", "waits", "key", "idx", "small", "force")


class View:
    __slots__ = ("ap", "space", "ranges")

    def __init__(self, ap, space, ranges):
        self.ap = ap
        self.space = space
        self.ranges = ranges

    def w(self, fn):
        return View(fn(self.ap), self.space, self.ranges)


def _merge(rs):
    rs = sorted(rs)
    out = []
    for lo, hi in rs:
        if out and lo <= out[-1][1]:
            out[-1][1] = max(out[-1][1], hi)
        else:
            out.append([lo, hi])
    return [(a, b) for a, b in out]


class Tens:
    def __init__(self, prog, name, shape, dtype, off=None, bank=None):
        self.shape = list(shape)
        self.es = 2 if dtype == BF16 else 4
        nc = prog.nc
        if bank is None:
            self.space = "s"
            self.off = off
            self.t = nc.alloc_sbuf_tensor_at(name, self.shape, dtype, offset=ARENA_BASE + off)
            n = 1
            for s in self.shape[1:]:
                n *= s
            assert off + n * self.es <= prog.arena_limit, (name, off, n * self.es)
        else:
            self.space = "p"
            self.off = bank * 2048
            self.t = nc.alloc_psum_tensor(name, self.shape, dtype)

    def __getitem__(self, key):
        if not isinstance(key, tuple):
            key = (key,)
        ap = self.t[key]
        if self.space == "p":
            return View(ap, "p", [(self.off, self.off + 2048)])
        fs = self.shape[1:]
        k = list(key[1:]) + [slice(None)] * (len(fs) - (len(key) - 1))
        idx = []
        for kk, s in zip(k, fs):
            if isinstance(kk, int):
                idx.append((kk, kk + 1))
            else:
                lo, hi, st = kk.indices(s)
                assert st == 1
                idx.append((lo, hi))
        strides = [1] * len(fs)
        for i in range(len(fs) - 2, -1, -1):
            strides[i] = strides[i + 1] * fs[i + 1]
        rs = [(0, 0)]
        starts = [0]
        for (lo, hi), st in zip(idx[:-1], strides[:-1]):
            starts = [s + j * st for s in starts for j in range(lo, hi)]
        lo, hi = idx[-1]
        rs = [(self.off + (s + lo) * self.es, self.off + (s + hi) * self.es) for s in starts]
        return View(ap, "s", _merge(rs))


class Prog:
    ENGS = ("pe", "act", "dve", "pool", "sp")

    def __init__(self, nc, arena_limit):
        self.nc = nc
        self.arena_limit = arena_limit
        self.ops = []
        self.recs = {"s": [], "p": []}
        self.nbank = 0

    def _record(self, op, reads, writes):
        deps = set()
        acc = []
        for v in reads:
            if isinstance(v, View):
                for r in v.ranges:
                    acc.append((v.space, r[0], r[1], False))
        op.small = False
        for v in writes:
            if isinstance(v, View):
                if v.space == "s" and sum(r[1] - r[0] for r in v.ranges) < 1024:
                    op.small = True
                for r in v.ranges:
                    acc.append((v.space, r[0], r[1], True))
        for sp, lo, hi, isw in acc:
            for rec in self.recs[sp]:
                if rec[0] < hi and lo < rec[1]:
                    if isw or rec[3]:
                        deps.add(rec[2])
                    elif sp == "p" and rec[2].eng != op.eng:
                        deps.add(rec[2])
        deps.discard(op)
        op.deps = deps
        for sp, lo, hi, isw in acc:
            recs = self.recs[sp]
            if isw:
                recs[:] = [r for r in recs if not (lo <= r[0] and r[1] <= hi)]
                recs.append((lo, hi, op, True))
            else:
                done = False
                for i, r in enumerate(recs):
                    if r[0] == lo and r[1] == hi and not r[3] and r[2].eng == op.eng and r[2].kind == "c" and op.kind == "c":
                        recs[i] = (lo, hi, op, False)
                        done = True
                        break
                if not done:
                    recs.append((lo, hi, op, False))

    def add(self, eng, fn, reads=(), writes=(), kind="c", key=None, extra=()):
        op = Op()
        op.eng, op.kind, op.fn, op.key = eng, kind, fn, key
        op.signal = False
        op.ev = None
        op.waits = []
        op.idx = len(self.ops)
        op.force = set()
        self._record(op, reads, writes)
        op.deps |= set(extra)
        if eng == "pe":
            if self.pe_fence and self.last_pe is not None:
                op.deps.add(self.last_pe)
                op.force.add(self.last_pe)
                self.pe_fence = False
            self.last_pe = op
        self.ops.append(op)
        return op

    rot = 8
    pe_fence = False
    last_pe = None

    def pe_barrier(self):
        self.pe_fence = True

    def next_bank(self):
        b = self.banks[self.nbank % self.rot]
        self.nbank += 1
        return b

    def emit(self):
        nc = self.nc
        for op in self.ops:
            for d in op.deps:
                if d.kind == "dma" or d.eng != op.eng or d.small or op.eng != "pe" or d in op.force:
                    d.signal = True
        esem = {e: nc.alloc_semaphore("sem_" + e) for e in self.ENGS}
        ecnt = {e: 0 for e in self.ENGS}
        dsem = {}
        dcnt = {}
        for op in self.ops:
            if op.kind == "dma":
                if op.key not in dsem:
                    dsem[op.key] = nc.alloc_semaphore("d_" + str(op.key))
                    dcnt[op.key] = 0
                dcnt[op.key] += 16
                op.ev = (dsem[op.key], dcnt[op.key])
            elif op.signal:
                ecnt[op.eng] += 1
                op.ev = (esem[op.eng], ecnt[op.eng])
        seen = {e: {} for e in self.ENGS}
        for op in self.ops:
            w = {}
            for d in op.deps:
                if d.kind != "dma" and d.eng == op.eng and not d.small and op.eng == "pe" and d not in op.force:
                    continue
                s, v = d.ev
                if seen[op.eng].get(s, 0) >= v:
                    continue
                if w.get(s, (None, 0))[1] < v:
                    w[s] = (s, v)
            for s, v in w.values():
                seen[op.eng][s] = v
            op.waits = list(w.values())
        per = {e: [o for o in self.ops if o.eng == e] for e in self.ENGS}

        def run(engobj, ops):
            for op in ops:
                for s, v in op.waits:
                    engobj.wait_ge(s, v)
                if op.fn is None:
                    continue
                ins = op.fn(engobj)
                if op.kind == "dma":
                    ins.then_inc(op.ev[0], 16)
                elif op.signal:
                    ins.then_inc(op.ev[0], 1)

        with nc.Block() as block:
            @block.sync
            def _(e):
                run(e, per["sp"])

            @block.gpsimd
            def _(e):
                run(e, per["pool"])

            @block.tensor
            def _(e):
                run(e, per["pe"])

            @block.vector
            def _(e):
                run(e, per["dve"])

            @block.scalar
            def _(e):
                run(e, per["act"])


def _ap(x):
    return x.ap if isinstance(x, View) else x


def build_program(debug=None):
    dumps = []
    nc = bass.Bass("TRN2", target_bir_lowering=False)
    limit = nc.sbuf_top - ARENA_BASE
    P = Prog(nc, limit)
    P.banks = [Tens(P, f"bank{i}", [128, 512], F32, bank=i) for i in range(8)]

    def din(name, shape, dt=F32):
        return nc.dram_tensor(name, list(shape), dt, kind="ExternalInput").ap()

    def dout(name, shape):
        return nc.dram_tensor(name, list(shape), F32, kind="ExternalOutput").ap()

    xp_d = din("xp", [512, D])
    xs_d = din("xs", [640, D])
    ck_d = din("ck", [512, 1024])
    cv_d = din("cv", [512, 1024])
    wada_d = din("w_ada", [D, 6 * D])
    win_d = din("w_in", [D, 4096])
    wout_d = din("w_out", [D, D])
    wg_d = din("w_gate", [D, DFF])
    wu_d = din("w_up", [D, DFF])
    wd_d = din("w_down", [DFF, D])
    wp_d = din("w_pool", [4, 256, 256])
    vecs_d = din("vecs", [128, 184])
    rows_d = din("rows", [128, 256])
    ident_d = din("ident", [128, 128])
    rpbg_d = din("rpbg", [8, 128, 4, 256])
    mask_d = din("maskneg", [128, 4, 256])
    As_d = din("poolA_s", [128, 5, 4, 256], BF16)
    Ap_d = din("poolA_p", [128, 2, 4, 256], BF16)
    invs_d = din("inv_s", [128, 4, 256])
    invp_d = din("inv_p", [128, 4, 256])
    yp_d = dout("yp", [512, D])
    ys_d = dout("ys", [256, D])
    nk_d = dout("nk", [512, 1024])
    nv_d = dout("nv", [512, 1024])

    def finish(extra_deps):
        stores = list(extra_deps)
        for name, view, shape, dt in dumps:
            dd = nc.dram_tensor(name, list(shape), dt, kind="ExternalOutput").ap()
            stores.append(dma("sp", dd, view, "dbg_" + name))
        P.add("sp", None, extra=stores)
        P.emit()
        return nc

    def dma(q, out, in_, key, reads=None, writes=None):
        r = [in_] if reads is None else reads
        w = [out] if writes is None else writes
        o, i = _ap(out), _ap(in_)
        return P.add(q, lambda e: e.dma_start(out=o, in_=i), reads=r, writes=w, kind="dma", key=key)

    def mm(out, lhsT, rhs, start, stop):
        o, l, r = _ap(out), _ap(lhsT), _ap(rhs)
        return P.add("pe", lambda e: e.matmul(o, lhsT=l, rhs=r, start=start, stop=stop), reads=[lhsT, rhs], writes=[out])

    def tr(out, in_, idv):
        o, i, d = _ap(out), _ap(in_), _ap(idv)
        return P.add("pe", lambda e: e.transpose(o, i, d), reads=[in_, idv], writes=[out])

    def act(out, in_, func, scale=1.0, bias=0.0, eng="act"):
        o, i, s, b = _ap(out), _ap(in_), _ap(scale), _ap(bias)
        return P.add("act", lambda e: e.activation(out=o, in_=i, func=func, bias=b, scale=s),
                     reads=[in_, scale, bias], writes=[out])

    def tt(out, in0, in1, op, eng="dve"):
        o, a, b = _ap(out), _ap(in0), _ap(in1)
        return P.add(eng, lambda e: e.tensor_tensor(out=o, in0=a, in1=b, op=op), reads=[in0, in1], writes=[out])

    def ts(out, in0, s1, s2, op0, op1=None, eng="dve"):
        o, a, x1, x2 = _ap(out), _ap(in0), _ap(s1), _ap(s2)
        if op1 is None:
            return P.add(eng, lambda e: e.tensor_scalar(out=o, in0=a, scalar1=x1, scalar2=None, op0=op0),
                         reads=[in0, s1], writes=[out])
        return P.add(eng, lambda e: e.tensor_scalar(out=o, in0=a, scalar1=x1, scalar2=x2, op0=op0, op1=op1),
                     reads=[in0, s1, s2], writes=[out])

    def stt(out, in0, scalar, in1, op0, op1, accum=None):
        o, a, s, b, ac = _ap(out), _ap(in0), _ap(scalar), _ap(in1), _ap(accum)
        if accum is None:
            return P.add("dve", lambda e: e.scalar_tensor_tensor(out=o, in0=a, scalar=s, in1=b, op0=op0, op1=op1),
                         reads=[in0, scalar, in1], writes=[out])
        return P.add("dve", lambda e: e.scalar_tensor_tensor(out=o, in0=a, scalar=s, in1=b, op0=op0, op1=op1,
                                                             accum_out=ac),
                     reads=[in0, scalar, in1], writes=[out, accum])

    def cp(out, in_, eng="dve"):
        o, i = _ap(out), _ap(in_)
        return P.add(eng, lambda e: e.tensor_copy(out=o, in_=i), reads=[in_], writes=[out])

    def recip(out, in_):
        o, i = _ap(out), _ap(in_)
        return P.add("dve", lambda e: e.reciprocal(out=o, in_=i), reads=[in_], writes=[out])

    def memset(out, val, eng="dve"):
        o = _ap(out)
        return P.add(eng, lambda e: e.memset(o, val), writes=[out])

    def flat(v):
        nd = len(v.ap.shape)
        if nd == 3:
            return v.w(lambda a: a.rearrange("p a b -> p (a b)"))
        return v

    def v3(v, a):
        return v.w(lambda x: x.rearrange("p (a b) -> p a b", a=a))

    def bc_last(v, shape):
        return v.w(lambda x: x.unsqueeze(2).to_broadcast(shape))

    def bc_mid(v, shape):
        return v.w(lambda x: x.unsqueeze(1).to_broadcast(shape))

    cpos = [0]

    def const(name, shape, dt):
        n = 1
        for s in shape[1:]:
            n *= s
        n *= 2 if dt == BF16 else 4
        off = cpos[0]
        cpos[0] = (off + n + 31) // 32 * 32
        assert cpos[0] <= 8 * KB
        return Tens(P, name, shape, dt, off=off)

    vecs = const("vecs", [128, 184], F32)
    rows = const("rows", [128, 256], F32)
    ident = const("ident", [128, 128], F32)
    onesf = const("onesf", [128, 128], F32)
    onesb = const("onesb", [128, 128], BF16)
    modT = const("modT", [128, 96, 2], F32)
    gsT = const("gsT", [128, 2, 16, 2], F32)
    sT = const("sT", [128, 16, 2], BF16)
    ssq = const("ssq", [128, 16], F32)
    rsq = const("rsq", [128, 16], F32)
    rstd = const("rstd", [128, 16], F32)
    ssq4 = const("ssq4", [128, 6, 4], F32)
    rsq4 = const("rsq4", [128, 6, 4], F32)
    rstd4 = const("rstd4", [128, 6, 4], F32)
    junkh = const("junkh", [128, 128], BF16)
    mrow = [const(f"mrow{i}", [128, 256], F32) for i in range(2)]

    wada = [Tens(P, f"wada{i}", [128, 16, 512], BF16, off=(8 + 16 * i) * KB) for i in range(3)]
    hT = Tens(P, "hT", [128, 16, 1152], BF16, off=8 * KB)
    NQ = 6
    qst = [Tens(P, f"qst{i}", [128, 4, 128], F32, off=(44 + 2 * i) * KB) for i in range(NQ)]
    attn_g = Tens(P, "attn_g", [128, 8, 256], F32, off=8 * KB)
    pool_g = Tens(P, "pool_g", [128, 8, 256], F32, off=16 * KB)
    Ep = [Tens(P, f"Ep{i}", [128, 2, 256], BF16, off=(24 + i) * KB) for i in range(8)]
    Es = [Tens(P, f"Es{i}", [128, 8, 256], BF16, off=(24 + 4 * i) * KB) for i in range(2)]
    maskneg = Tens(P, "maskneg", [128, 4, 256], F32, off=32 * KB)
    rpbg = [Tens(P, f"rpbg{i}", [128, 4, 256], F32, off=(36 + 4 * i) * KB) for i in range(2)]
    bm = Tens(P, "bm", [128, 4, 256], F32, off=44 * KB)
    dT = Tens(P, "dT", [128, 8, 256], BF16, off=48 * KB)
    rden = [Tens(P, f"rden{i}", [128, 256], F32, off=(52 + i) * KB) for i in range(2)]
    rsn = Tens(P, "rsn", [128, 256], F32, off=54 * KB)
    rstdn = Tens(P, "rstdn", [128, 256], F32, off=55 * KB)
    x1 = Tens(P, "x1", [128, 6, D], F32, off=8 * KB)
    qT = Tens(P, "qT", [128, 8, 768], BF16, off=56 * KB)
    kT = Tens(P, "kT", [128, 8, 1024], BF16, off=68 * KB)
    V = Tens(P, "V", [128, 8, 1024], BF16, off=84 * KB)
    U = Tens(P, "U", [128, 9, 1024], BF16, off=100 * KB)
    ckT = Tens(P, "ckT", [128, 8, 512], BF16, off=118 * KB)
    CV = Tens(P, "CV", [128, 4, 1024], BF16, off=126 * KB)
    junk2 = Tens(P, "junk2", [128, D], BF16, off=104 * KB)
    diag2 = [Tens(P, f"diag2_{i}", [128, 128], F32, off=120 * KB + 512 * i) for i in range(6)]
    xa = [Tens(P, f"xa{i}", [128, D], F32, off=o * KB) for i, o in enumerate([56, 64, 72, 80, 88, 96, 104, 134, 142])]
    junkA = Tens(P, "junkA", [128, D], BF16, off=112 * KB)
    diagA = [Tens(P, f"diagA_{i}", [128, 128], F32, off=150 * KB + 512 * i) for i in range(8)]
    diagA.append(Tens(P, "diagA_8", [128, 128], F32, off=186 * KB))
    attn_g2 = Tens(P, "attn_g2", [128, 8, 256], F32, off=32 * KB)
    pool_g2 = Tens(P, "pool_g2", [128, 8, 256], F32, off=40 * KB)
    aT = Tens(P, "aT", [128, NF, 768], BF16, off=56 * KB)
    woutr = [Tens(P, f"woutr{i}", [128, 16, 512], BF16, off=(56 + 16 * i) * KB) for i in range(2)]
    gbc1 = Tens(P, "gbc1", [128, 2, D], F32, off=88 * KB)
    wdr = [Tens(P, f"wdr{i}", [128, 4, 512], BF16, off=o * KB) for i, o in enumerate([158, 162, 166, 170, 174, 178])]
    scr = [Tens(P, f"scr{i}", [128, 512], F32, off=(124 + 2 * i) * KB) for i in range(4)]
    diag = [Tens(P, f"diag{i}", [128, 128], F32, off=132 * KB + 512 * i) for i in range(2)]
    winr = [Tens(P, f"winr{i}", [128, 16, 512], BF16, off=(154 + 16 * i) * KB) for i in range(2)]
    vst = [Tens(P, f"vst{i}", [128, 512], F32, off=(186 + 2 * i) * KB) for i in range(2)]
    NW2 = 4
    wada2 = [Tens(P, f"wada2_{i}", [128, 16, 128], BF16, off=(190 + 4 * i) * KB) for i in range(NW2)]
    ycatT = Tens(P, "ycatT", [128, 16, 768], BF16, off=134 * KB)
    sqbuf = Tens(P, "sqbuf", [128, 8, 256], F32, off=158 * KB)
    As = Tens(P, "As", [128, 5, 4, 256], BF16, off=166 * KB)
    Ap = Tens(P, "Ap", [128, 2, 4, 256], BF16, off=176 * KB)
    invs = Tens(P, "invs", [128, 4, 256], F32, off=176 * KB)
    invp = Tens(P, "invp", [128, 4, 256], F32, off=184 * KB)
    wpool = Tens(P, "wpool", [128, 4, 2, 256], BF16, off=180 * KB)
    tmpS = Tens(P, "tmpS", [128, 512], F32, off=184 * KB)
    h2T = Tens(P, "h2T", [128, 16, 768], BF16, off=134 * KB)
    gbc2 = Tens(P, "gbc2", [128, 2, D], F32, off=134 * KB)
    NFR = 4
    ffr = [(Tens(P, f"wg{i}", [128, 16, 128], BF16, off=(158 + 8 * i) * KB),
            Tens(P, f"wu{i}", [128, 16, 128], BF16, off=(162 + 8 * i) * KB)) for i in range(NFR)]

    def vcol(lo, hi):
        return vecs[:, lo:hi]

    C_COND, C_BADA, C_N1G, C_N2G, C_PSC, C_OAG, C_OPG = 0, 32, 128, 144, 160, 168, 176
    wada_v = wada_d.rearrange("(c p) e -> p c e", p=128)
    win_v = win_d.rearrange("(c p) e -> p c e", p=128)

    dma("sp", vecs[:, :], vecs_d[:, :], "c_vecs")
    dma("sp", rows[:, :], rows_d[:, :], "c_rows")
    dma("sp", ident[:, :], ident_d[:, :], "c_ident")
    memset(onesf[:, :], 1.0)
    memset(onesb[:, :], 1.0)
    act(sT[:, :, :], v3(vcol(C_COND, C_COND + 32), 16), AF.Silu)

    NPRE_A = 3
    for blk in range(NPRE_A):
        dma("pool", wada[blk % 3][:, :, :], wada_v[:, :, blk * 512:(blk + 1) * 512], f"wada{blk % 3}")
    dma("pool", CV[:, :, :], cv_d.rearrange("(t p) e -> p t e", p=128), "c_cv")
    for t in range(7):
        src = xp_d[t * 128:(t + 1) * 128, :] if t < 4 else xs_d[(t - 4) * 128:(t - 3) * 128, :]
        dma("sp", xa[t][:, :], src, f"xa{t}")
    for t in range(4):
        st_ = xa[7 + t % 2]
        dma("sp", st_[:, 0:1024], ck_d[t * 128:(t + 1) * 128, :], f"xa{7 + t % 2}")
        for hb in range(2):
            bk = P.next_bank()
            for hh in range(4):
                h = hb * 4 + hh
                tr(bk[:, hh * 128:(hh + 1) * 128], st_[:, h * 128:(h + 1) * 128], ident[:, :])
            act(ckT[:, hb * 4:(hb + 1) * 4, t * 128:(t + 1) * 128], v3(bk[:, :], 4), AF.Copy)

    WIN_ORDER = [2, 3, 4, 5, 6, 7, 0, 1]

    def load_win(i):
        b = WIN_ORDER[i]
        dma("pool", winr[i % 2][:, :, :], win_v[:, :, b * 512:(b + 1) * 512], f"winr{i % 2}")

    bmod = P.banks[7]
    P.rot = 7

    def mod_evac(a, b):
        tt(modT[:, a:b, :], v3(bmod[:, 2 * a:2 * b], b - a), bc_last(vcol(C_BADA + a, C_BADA + b), [128, b - a, 2]),
           ALU.add)

    def make_gs(n, sc_lo, g_lo):
        ts(gsT[:, n, :, :], modT[:, sc_lo:sc_lo + 16, :], 1.0, None, ALU.add)
        tt(gsT[:, n, :, :], gsT[:, n, :, :], bc_last(vcol(g_lo, g_lo + 16), [128, 16, 2]), ALU.mult)

    for blk in range(8):
        wsl = wada[blk % 3]
        for jj in range(4):
            j = blk * 4 + jj
            for c in range(NCH):
                mm(bmod[:, j * 2:(j + 1) * 2], wsl[:, c, jj * 128:(jj + 1) * 128], sT[:, c, :], c == 0, c == NCH - 1)
        if blk + NPRE_A < 8:
            nb = blk + NPRE_A
            dma("pool", wada[nb % 3][:, :, :], wada_v[:, :, nb * 512:(nb + 1) * 512], f"wada{nb % 3}")
    P.nbank = 0
    mod_evac(0, 32)
    make_gs(0, 16, C_N1G)
    load_win(0)
    load_win(1)

    def ada2_gen():
        def ld(j):
            dma("pool", wada2[j % NW2][:, :, :], wada_v[:, :, j * 128:(j + 1) * 128], f"wada2_{j % NW2}")
        for j in range(32, 32 + NW2):
            ld(j)
        for j in range(32, 96):
            wsl = wada2[j % NW2]
            for c in range(NCH):
                mm(bmod[:, j * 2:(j + 1) * 2], wsl[:, c, :], sT[:, c, :], c == 0, c == NCH - 1)
            if j + NW2 < 96:
                ld(j + NW2)
            if j == 47:
                mod_evac(32, 48)
            elif j == 79:
                mod_evac(48, 80)
                make_gs(1, 64, C_N2G)
            elif j == 95:
                mod_evac(80, 96)
                P.rot = 8
            yield

    ada2 = ada2_gen()

    def ada2_step(n=1):
        for _ in range(n):
            next(ada2, None)

    def cond_of(t):
        return 0 if t < 4 else 1

    def rstd_chain(ssq_v, rsq_v, rstd_v, n):
        ts(rsq_v, ssq_v, 1.0 / n, EPS, ALU.mult, ALU.add)
        act(rsq_v, rsq_v, AF.Ln)
        act(rstd_v, rsq_v, AF.Exp, scale=-0.5)

    def norm_stats(src_view, junk, col, dg):
        P.add("act", (lambda e, o=_ap(junk[:, :]), i=_ap(src_view), a=_ap(ssq[:, col:col + 1]):
                      e.activation(out=o, in_=i, func=AF.Square, accum_out=a)),
              reads=[src_view], writes=[junk[:, :], ssq[:, col:col + 1]])
        rstd_chain(ssq[:, col:col + 1], rsq[:, col:col + 1], rstd[:, col:col + 1], D)
        ts(dg[:, :], ident[:, :], rstd[:, col:col + 1], None, ALU.mult)

    evi = [0]
    nstep = [0]

    def norm_group(srcs, dgs, n_idx, sh_lo, dst, dcol, cnd):
        nt = len(srcs)
        for c in range(NCH):
            bk = P.next_bank()
            for k in range(nt):
                mm(bk[:, k * 128:(k + 1) * 128], srcs[k](c), dgs[k][:, :], True, True)
            o = dst[:, c, dcol:dcol + nt * 128]
            i = bk[:, 0:nt * 128]
            g_ = gsT[:, n_idx, c, cnd:cnd + 1]
            s_ = modT[:, sh_lo + c, cnd:cnd + 1]
            evi[0] += 1
            if evi[0] % 2 == 0:
                act(o, i, AF.Identity, scale=g_, bias=s_)
            else:
                ts(o, i, g_, s_, ALU.mult, ALU.add)
            if False:
                ada2_step()
                nstep[0] += 1

    for t in range(7):
        norm_stats(xa[t][:, :], junkA, t, diagA[t])
    for t in range(7, 9):
        dma("sp", xa[t][:, :], xs_d[(t - 4) * 128:(t - 3) * 128, :], f"xa{t}")
        norm_stats(xa[t][:, :], junkA, t, diagA[t])

    def xsrc(t):
        return lambda c: xa[t][:, c * 128:(c + 1) * 128]

    norm_group([xsrc(t) for t in range(4)], diagA[0:4], 0, 0, hT, 0, 0)
    norm_group([xsrc(t) for t in range(4, 8)], diagA[4:8], 0, 0, hT, 512, 1)
    norm_group([xsrc(8)], diagA[8:9], 0, 0, hT, 1024, 1)

    if debug == "A":
        dumps.append(("dbg_modT", flat(modT[:, :, :]), [128, 192], F32))
        dumps.append(("dbg_hT", flat(hT[:, :, :]), [128, 16 * 1152], BF16))
        dumps.append(("dbg_rstd", rstd[:, :], [128, 16], F32))
        dumps.append(("dbg_ssq", ssq[:, :], [128, 16], F32))
        dumps.append(("dbg_ckT", flat(ckT[:, :, :]), [128, 8 * 512], BF16))
        return finish([])
    kv_stores = []
    qi = [0]
    vi = [0]
    pending = []
    TR_DELAY = 3
    bgrp = [0]

    def flush_pending(keep):
        while len(pending) > keep:
            pending.pop(0)()

    for i in range(8):
        b = WIN_ORDER[i]
        wsl = winr[i % 2]
        kind = "qkvu"[b // 2]
        half = b % 2
        tiles = {"q": [0, 1, 2, 3, 4, 5], "k": list(range(8)), "v": list(range(8)), "u": list(range(9))}[kind]
        for t in tiles:
            bk = P.next_bank()
            for c in range(NCH):
                mm(bk[:, :], hT[:, c, t * 128:(t + 1) * 128], wsl[:, c, :], c == 0, c == NCH - 1)
            bgrp[0] += 1
            if nstep[0] < 36 and bgrp[0] % 2 == 0:
                ada2_step()
                nstep[0] += 1
            flush_pending(TR_DELAY)
            if kind in "qk":
                s = qi[0] % NQ
                qi[0] += 1
                st_ = qst[s]
                act(flat(st_[:, :, :]), bk[:, :], AF.Copy)
                for hh in range(4):
                    stt(junkh[:, :], st_[:, hh, :], 1.0, st_[:, hh, :], ALU.mult, ALU.mult,
                        accum=ssq4[:, s, hh:hh + 1])
                rstd_chain(ssq4[:, s, :], rsq4[:, s, :], rstd4[:, s, :], 128)
                tt(st_[:, :, :], st_[:, :, :], bc_last(rstd4[:, s, :], [128, 4, 128]), ALU.mult)
                goff = 0 if kind == "q" else 128
                tt(st_[:, :, :], st_[:, :, :], bc_mid(rows[:, goff:goff + 128], [128, 4, 128]), ALU.mult)
                if kind == "k" and t < 4:
                    kv_stores.append(dma("sp", nk_d[t * 128:(t + 1) * 128, half * 512:(half + 1) * 512],
                                         flat(st_[:, :, :]), f"qst{s}"))

                def job(st_=st_, kind=kind, half=half, t=t):
                    bk2 = P.next_bank()
                    for hh in range(4):
                        tr(bk2[:, hh * 128:(hh + 1) * 128], st_[:, hh, :], ident[:, :])
                    dstT = qT if kind == "q" else kT
                    act(dstT[:, half * 4:(half + 1) * 4, t * 128:(t + 1) * 128], v3(bk2[:, :], 4), AF.Copy)
                pending.append(job)
            elif kind == "v":
                act(V[:, t, half * 512:(half + 1) * 512], bk[:, :], AF.Copy)
                if t < 4:
                    s = vi[0] % 2
                    vi[0] += 1
                    cp(vst[s][:, :], bk[:, :])
                    kv_stores.append(dma("sp", nv_d[t * 128:(t + 1) * 128, half * 512:(half + 1) * 512],
                                         vst[s][:, :], f"vst{s}"))
            else:
                act(U[:, t, half * 512:(half + 1) * 512], bk[:, :], AF.Copy)
        if i + 2 < 8:
            load_win(i + 2)
    flush_pending(0)

    if debug == "B":
        dumps.append(("dbg_qT", flat(qT[:, :, :]), [128, 8 * 768], BF16))
        dumps.append(("dbg_kT", flat(kT[:, :, :]), [128, 8 * 1024], BF16))
        dumps.append(("dbg_V", flat(V[:, :, :]), [128, 8 * 1024], BF16))
        dumps.append(("dbg_U", flat(U[:, :, :]), [128, 9 * 1024], BF16))
        return finish(kv_stores)
    dma("sp", As[:, :, :, :], As_d[:, :, :, :], "c_As")
    dma("sp", Ap[:, :, :, :], Ap_d[:, :, :, :], "c_Ap")
    dma("sp", invp[:, :, :], invp_d[:, :, :], "c_invp")
    dma("pool", wpool[:, :, :, :], wp_d.rearrange("g (k p) e -> p g k e", p=128), "c_wpool")

    epi = [0]
    rdi = [0]

    def attn_gen(G, attn_dst):
        q0 = G * 256
        pend = []
        LA = 3 if G < 2 else 1

        def front(h):
            if G < 2:
                E = Ep[epi[0] % 8]
                epi[0] += 1
                bS = P.next_bank()
                for i in range(2):
                    mm(bS[:, i * 256:(i + 1) * 256], kT[:, h, q0 + i * 128:q0 + (i + 1) * 128], qT[:, h, q0:q0 + 256],
                       True, True)
                act(flat(E[:, 0:2, :]), bS[:, :], AF.Exp, scale=SCALE)
                pv = [(V[:, 2 * G + i, h * 128:(h + 1) * 128], E[:, i, :]) for i in range(2)]
            else:
                E = Es[h % 2]
                r = rpbg[h % 2]
                dma("sp", r[:, :, :], rpbg_d[h], f"rpbg{h % 2}")
                tt(bm[:, :, :], r[:, :, :], maskneg[:, :, :], ALU.add, eng="pool")
                bL = [P.next_bank(), P.next_bank()]
                for i in range(4):
                    mm(bL[i // 2][:, (i % 2) * 256:(i % 2 + 1) * 256], kT[:, h, 512 + i * 128:512 + (i + 1) * 128],
                       qT[:, h, 512:768], True, True)
                for j in range(2):
                    stt(tmpS[:, :], bL[j][:, :], SCALE, flat(bm[:, 2 * j:2 * j + 2, :]), ALU.mult, ALU.add)
                    act(flat(E[:, 2 * j:2 * j + 2, :]), tmpS[:, :], AF.Exp)
                bC = [P.next_bank(), P.next_bank()]
                for i in range(4):
                    mm(bC[i // 2][:, (i % 2) * 256:(i % 2 + 1) * 256], ckT[:, h, i * 128:(i + 1) * 128],
                       qT[:, h, 512:768], True, True)
                for j in range(2):
                    act(flat(E[:, 4 + 2 * j:6 + 2 * j, :]), bC[j][:, :], AF.Exp, scale=SCALE)
                pv = [(V[:, 4 + i, h * 128:(h + 1) * 128], E[:, i, :]) for i in range(4)]
                pv += [(CV[:, i, h * 128:(h + 1) * 128], E[:, 4 + i, :]) for i in range(4)]
            return (h, pv)

        def back(h, pv):
            bO = P.next_bank()
            n = len(pv)
            for i, (l, r_) in enumerate(pv):
                mm(bO[:, 0:256], l, r_, i == 0, i == n - 1)
            for i, (l, r_) in enumerate(pv):
                mm(bO[:, 256:512], onesb[:, :], r_, i == 0, i == n - 1)
            rd = rden[rdi[0] % 2]
            rdi[0] += 1
            act(rd[:, :], bO[:, 256:512], AF.Ln)
            act(rd[:, :], rd[:, :], AF.Exp, scale=-1.0)
            tt(attn_dst[:, h, :], bO[:, 0:256], rd[:, :], ALU.mult)

        for h in range(8):
            pend.append(front(h))
            if len(pend) > LA:
                back(*pend.pop(0))
            yield
        while pend:
            back(*pend.pop(0))
            yield

    def pool_gen(G, pool_dst):
        if G < 2:
            utiles, A, inv = [2 * G, 2 * G + 1], Ap, invp
        else:
            utiles, A, inv = [4, 5, 6, 7, 8], As, invs
        for cc2 in range(4):
            bD = P.next_bank()
            for k in range(2):
                cc = cc2 * 2 + k
                g = cc // 2
                for i, ut in enumerate(utiles):
                    mm(bD[:, k * 256:(k + 1) * 256], U[:, ut, cc * 128:(cc + 1) * 128], A[:, i, g, :], i == 0,
                       i == len(utiles) - 1)
            g = cc2
            tt(dT[:, 2 * cc2:2 * cc2 + 2, :], v3(bD[:, :], 2), bc_mid(inv[:, g, :], [128, 2, 256]), ALU.mult)
            yield
        for g in range(4):
            bY = P.next_bank()
            for eh in range(2):
                for k in range(2):
                    mm(bY[:, eh * 256:(eh + 1) * 256], wpool[:, g, k, eh * 128:(eh + 1) * 128], dT[:, 2 * g + k, :],
                       k == 0, k == 1)
            for eh in range(2):
                ee = 2 * g + eh
                act(pool_dst[:, ee, :], bY[:, eh * 256:(eh + 1) * 256], AF.Identity,
                    scale=vcol(C_PSC + ee, C_PSC + ee + 1))
            yield

    def merge_gen(G, a_src, p_src):
        q0 = G * 256
        for br, (src, gcol) in enumerate([(a_src, C_OAG), (p_src, C_OPG)]):
            tt(flat(sqbuf[:, :, :]), flat(src[:, :, :]), flat(src[:, :, :]), ALU.mult)
            yield
            bN = P.next_bank()
            for h in range(8):
                mm(bN[:, 0:256], onesf[:, :], sqbuf[:, h, :], h == 0, h == 7)
            ts(rsn[:, :], bN[:, 0:256], 1.0 / 1024, EPS, ALU.mult, ALU.add)
            act(rsn[:, :], rsn[:, :], AF.Ln)
            act(rstdn[:, :], rsn[:, :], AF.Exp, scale=-0.5)
            yield
            for h in range(8):
                stt(ycatT[:, br * 8 + h, q0:q0 + 256], src[:, h, :], vcol(gcol + h, gcol + h + 1), rstdn[:, :],
                    ALU.mult, ALU.mult)
                if h % 4 == 3:
                    yield

    bufs = [(attn_g, pool_g), (attn_g2, pool_g2), (attn_g, pool_g)]

    rnd = [0]

    def run_gens(alive):
        while alive:
            for g_ in list(alive):
                try:
                    next(g_)
                except StopIteration:
                    alive.remove(g_)
            rnd[0] += 1
            if nstep[0] < 48 and rnd[0] % 2 == 0:
                ada2_step()
                nstep[0] += 1

    run_gens([attn_gen(0, attn_g), pool_gen(0, pool_g)])
    run_gens([attn_gen(1, attn_g2), pool_gen(1, pool_g2), merge_gen(0, attn_g, pool_g)])
    run_gens([merge_gen(1, attn_g2, pool_g2)])
    dma("sp", maskneg[:, :, :], mask_d[:, :, :], "c_mask")
    dma("sp", invs[:, :, :], invs_d[:, :, :], "c_invs")
    run_gens([attn_gen(2, attn_g), pool_gen(2, pool_g)])
    run_gens([merge_gen(2, attn_g, pool_g)])
    while nstep[0] < 48:
        ada2_step()
        nstep[0] += 1

    if debug == "C":
        dumps.append(("dbg_ycatT", flat(ycatT[:, :, :]), [128, 16 * 768], BF16))
        dumps.append(("dbg_attn_g", flat(attn_g[:, :, :]), [128, 8 * 256], F32))
        dumps.append(("dbg_pool_g", flat(pool_g[:, :, :]), [128, 8 * 256], F32))
        return finish(kv_stores)
    def gate_bcast(gb, g_lo):
        di = 0
        for cnd in range(2):
            for cb in range(4):
                bk = P.next_bank()
                for k in range(4):
                    c = cb * 4 + k
                    dg = diag[di % 2]
                    di += 1
                    ts(dg[:, :], ident[:, :], modT[:, g_lo + c, cnd:cnd + 1], None, ALU.mult)
                    mm(bk[:, k * 128:(k + 1) * 128], onesf[:, :], dg[:, :], True, True)
                act(gb[:, cnd, cb * 512:(cb + 1) * 512], bk[:, :], AF.Copy)

    wout_v = wout_d.rearrange("(c p) e -> p c e", p=128)
    wg_v = wg_d.rearrange("(c p) e -> p c e", p=128)
    wu_v = wu_d.rearrange("(c p) e -> p c e", p=128)
    wi = [0]

    def load_wout(db):
        s_ = wi[0] % 2
        wi[0] += 1
        dma("pool", woutr[s_][:, :, :], wout_v[:, :, db * 512:(db + 1) * 512], f"woutr{s_}")
        return woutr[s_]

    def load_ff(i):
        wg_t, wu_t = ffr[i % NFR]
        dma("pool", wg_t[:, :, :], wg_v[:, :, i * 128:(i + 1) * 128], f"wg{i % NFR}")
        dma("pool", wu_t[:, :, :], wu_v[:, :, i * 128:(i + 1) * 128], f"wu{i % NFR}")

    wq = [load_wout(0), load_wout(1)]
    gate_bcast(gbc1, 32)
    for t in range(6):
        src = xp_d[t * 128:(t + 1) * 128, :] if t < 4 else xs_d[(t - 4) * 128:(t - 3) * 128, :]
        dma("sp", x1[:, t, :], src, f"x1_{t}")
    si = [0]

    def x1src(t):
        return lambda c: x1[:, t, c * 128:(c + 1) * 128]

    for db in range(4):
        wsl = wq.pop(0)
        for t in range(6):
            bk = P.next_bank()
            for e in range(16):
                mm(bk[:, :], ycatT[:, e, t * 128:(t + 1) * 128], wsl[:, e, :], e == 0, e == 15)
            s = scr[si[0] % 4]
            si[0] += 1
            tt(s[:, :], bk[:, :], gbc1[:, cond_of(t), db * 512:(db + 1) * 512], ALU.mult)
            tt(x1[:, t, db * 512:(db + 1) * 512], x1[:, t, db * 512:(db + 1) * 512], s[:, :], ALU.add, eng="pool")
            if db == 3:
                norm_stats(x1[:, t, :], junk2, 9 + t, diag2[t])
        if db + 2 < 4:
            wq.append(load_wout(db + 2))
        if db == 1:
            for i in range(NFR):
                load_ff(i)
    norm_group([x1src(t) for t in range(4)], diag2[0:4], 1, 48, h2T, 0, 0)
    norm_group([x1src(t) for t in range(4, 6)], diag2[4:6], 1, 48, h2T, 512, 1)

    if debug == "C3":
        dumps.append(("dbg_x1", flat(x1[:, :, :]), [128, 6 * D], F32))
        dumps.append(("dbg_gbc1", flat(gbc1[:, :, :]), [128, 2 * D], F32))
        dumps.append(("dbg_modT", flat(modT[:, :, :]), [128, 192], F32))
        return finish(kv_stores)
    for f in range(NF):
        wg_t, wu_t = ffr[f % NFR]
        bA, bB, bC_, bD_ = P.next_bank(), P.next_bank(), P.next_bank(), P.next_bank()
        for (w_t, b1, b2) in ((wg_t, bA, bB), (wu_t, bC_, bD_)):
            for c in range(NCH):
                mm(b1[:, :], w_t[:, c, :], h2T[:, c, 0:512], c == 0, c == NCH - 1)
                mm(b2[:, 0:256], w_t[:, c, :], h2T[:, c, 512:768], c == 0, c == NCH - 1)
        s = scr[si[0] % 4]
        si[0] += 1
        act(s[:, :], bA[:, :], AF.Silu)
        tt(aT[:, f, 0:512], s[:, :], bC_[:, :], ALU.mult)
        s = scr[si[0] % 4]
        si[0] += 1
        act(s[:, 0:256], bB[:, 0:256], AF.Silu)
        tt(aT[:, f, 512:768], s[:, 0:256], bD_[:, 0:256], ALU.mult)
        if f + NFR < NF:
            load_ff(f + NFR)
        if f >= 2:
            ada2_step()
    assert P.rot == 8

    def load_wd(i):
        db, fg = divmod(i, 11)
        dma("pool", wdr[i % 6][:, :, :],
            wd_d.rearrange("(f p) d -> p f d", p=128)[:, fg * 4:(fg + 1) * 4, db * 512:(db + 1) * 512], f"wdr{i % 6}")

    for i in range(6):
        load_wd(i)
    gate_bcast(gbc2, 80)
    ystores = []
    for db in range(4):
        acc = [P.next_bank() for _ in range(6)]
        for fg in range(11):
            i = db * 11 + fg
            wsl = wdr[i % 6]
            for fi in range(4):
                f = fg * 4 + fi
                for t in range(6):
                    mm(acc[t][:, :], aT[:, f, t * 128:(t + 1) * 128], wsl[:, fi, :], f == 0, f == NF - 1)
            if i + 6 < 44:
                load_wd(i + 6)
        for t in range(6):
            s = scr[si[0] % 4]
            si[0] += 1
            tt(s[:, :], acc[t][:, :], gbc2[:, cond_of(t), db * 512:(db + 1) * 512], ALU.mult)
            tt(x1[:, t, db * 512:(db + 1) * 512], x1[:, t, db * 512:(db + 1) * 512], s[:, :], ALU.add, eng="pool")
            dst = yp_d[t * 128:(t + 1) * 128, db * 512:(db + 1) * 512] if t < 4 else \
                ys_d[(t - 4) * 128:(t - 3) * 128, db * 512:(db + 1) * 512]
            ystores.append(dma("sp", dst, x1[:, t, db * 512:(db + 1) * 512], "ystore"))
    return finish(ystores + kv_stores)


def _sample_slots(cb):
    own = np.array([64 * r + 16 * cb + j for r in range(16) for j in range(16)], np.int64)
    ss = int(np.clip(16 * cb - 8, 0, 32))
    hcols = [c for c in range(ss, ss + 32) if not (16 * cb <= c < 16 * cb + 16)]
    halo = np.array([64 * r + c for r in range(16) for c in hcols], np.int64)
    extra = np.full(128, -1, np.int64)
    for r in range(16):
        for m in range(8):
            if cb == 0:
                tt_ = 64 * r - 8 + m
            elif cb == 3:
                tt_ = 64 * r + 64 + m
            else:
                tt_ = -1
            if 0 <= tt_ < 1024:
                extra[r * 8 + m] = tt_
    return own, halo, extra


def _pool_tables(slot_tok, own_tok, L):
    ns, no = len(slot_tok), len(own_tok)
    lookup = {int(t): s for s, t in enumerate(slot_tok) if t >= 0}
    A = np.zeros((ns, 4, no), np.float32)
    inv = np.zeros((4, no), np.float32)
    for g, w in enumerate((2, 4, 8, 16)):
        for o, T in enumerate(own_tok):
            lo = max(int(T) - w // 2, 0)
            hi = min(int(T) + w // 2, L)
            cnt = hi - lo
            for t2 in range(lo, hi):
                A[lookup[t2], g, o] += 1.0
            A[lookup[int(T)], g, o] -= cnt
            inv[g, o] = 1.0 / cnt
    return A, inv


def _bias_tables(cb, own, halo):
    keys = np.concatenate([own, halo])
    kr, kc = keys // 64, keys % 64
    qr, qc = own // 64, own % 64
    rs = np.clip(qr - 4, 0, 8)
    cs = np.clip(qc - 8, 0, 48)
    valid = (kr[:, None] >= rs[None, :]) & (kr[:, None] < rs[None, :] + 8) & \
            (kc[:, None] >= cs[None, :]) & (kc[:, None] < cs[None, :] + 16)
    dr = np.clip(kr[:, None] - qr[None, :] + 7, 0, 14)
    dc = np.clip(kc[:, None] - qc[None, :] + 15, 0, 30)
    return valid, dr, dc


_NC_CACHE = {}


def kernel(x_prompt, x_sample, cache_k, cache_v, c, c_ctx, w_ada, b_ada, norm1_g, w_in, q_norm_g, k_norm_g, rpb,
           w_pool, pool_scale, out_norm_attn_g, out_norm_pool_g, w_out, norm2_g, w_gate, w_up, w_down):
    f32 = np.float32
    A = lambda a: np.ascontiguousarray(np.asarray(a, dtype=f32))
    x_prompt, x_sample, cache_k, cache_v = A(x_prompt), A(x_sample), A(cache_k), A(cache_v)
    c, c_ctx, b_ada = A(c), A(c_ctx), A(b_ada)
    shared = {
        "w_ada": A(w_ada)[0], "w_in": A(w_in)[0], "w_out": A(w_out)[0], "w_gate": A(w_gate)[0],
        "w_up": A(w_up)[0], "w_down": A(w_down)[0], "w_pool": A(w_pool)[0],
        "ident": np.eye(128, dtype=f32),
    }
    rows = np.ascontiguousarray(np.broadcast_to(
        np.concatenate([A(q_norm_g)[0], A(k_norm_g)[0]])[None, :], (128, 256)))
    shared["rows"] = rows

    def pl(v, n):
        return np.asarray(v, f32).reshape(n, 128).T

    tokp = np.arange(256)
    Ap_, invp_ = _pool_tables(tokp, tokp, 256)
    shared["poolA_p"] = np.ascontiguousarray(Ap_.reshape(2, 128, 4, 256).transpose(1, 0, 2, 3)).astype(ml_dtypes.bfloat16)
    shared["inv_p"] = np.ascontiguousarray(np.broadcast_to(invp_[None], (128, 4, 256)))
    rpb0 = A(rpb)[0]

    in_maps = []
    owns = []
    for i in range(NCORES):
        b, cb = i // 4, i % 4
        own, halo, extra = _sample_slots(cb)
        owns.append(own)
        slots = np.concatenate([own, halo, extra])
        xs = np.zeros((640, D), f32)
        ok = slots >= 0
        xs[ok] = x_sample[b][slots[ok]]
        As_, invs_ = _pool_tables(slots, own, 1024)
        valid, dr, dc = _bias_tables(cb, own, halo)
        rg = rpb0[:, dr, dc]
        rg = np.where(valid[None], rg, f32(0.0))
        mneg = np.where(valid, f32(0.0), f32(NEG)).astype(f32)
        vecs = np.concatenate([
            np.stack([pl(c_ctx, 16), pl(c[b], 16)], axis=2).reshape(128, 32),
            pl(b_ada[0], 96), pl(A(norm1_g)[0], 16), pl(A(norm2_g)[0], 16), pl(A(pool_scale)[0], 8),
            pl(A(out_norm_attn_g)[0], 8), pl(A(out_norm_pool_g)[0], 8)], axis=1)
        m = dict(shared)
        m.update({
            "xp": x_prompt[2 * i:2 * i + 2].reshape(512, D),
            "xs": xs,
            "ck": cache_k[b, 0].reshape(512, 1024),
            "cv": cache_v[b, 0].reshape(512, 1024),
            "vecs": np.ascontiguousarray(vecs.astype(f32)),
            "rpbg": np.ascontiguousarray(rg.reshape(8, 4, 128, 256).transpose(0, 2, 1, 3).astype(f32)),
            "maskneg": np.ascontiguousarray(mneg.reshape(4, 128, 256).transpose(1, 0, 2)),
            "poolA_s": np.ascontiguousarray(As_.reshape(5, 128, 4, 256).transpose(1, 0, 2, 3)).astype(ml_dtypes.bfloat16),
            "inv_s": np.ascontiguousarray(np.broadcast_to(invs_[None], (128, 4, 256))),
        })
        in_maps.append(m)

    import os
    dbg = os.environ.get("KDEBUG")
    if dbg:
        nc = build_program(debug=dbg)
        res = run_bass_kernel_spmd(nc, in_maps[:1], core_ids=[0])
        return res.results[0], in_maps[0]
    if "nc" not in _NC_CACHE:
        _NC_CACHE["nc"] = build_program()
    nc = _NC_CACHE["nc"]
    res = run_bass_kernel_spmd(nc, in_maps, core_ids=list(range(NCORES)))
    y_prompt = np.zeros((16, 256, D), f32)
    y_sample = np.zeros((2, 1024, D), f32)
    nk = np.zeros((16, 1, 256, 8, 128), f32)
    nv = np.zeros((16, 1, 256, 8, 128), f32)
    for i in range(NCORES):
        r = res.results[i]
        y_prompt[2 * i:2 * i + 2] = np.asarray(r["yp"]).reshape(2, 256, D)
        y_sample[i // 4][owns[i]] = np.asarray(r["ys"])
        nk[2 * i:2 * i + 2, 0] = np.asarray(r["nk"]).reshape(2, 256, 8, 128)
        nv[2 * i:2 * i + 2, 0] = np.asarray(r["nv"]).reshape(2, 256, 8, 128)
    return (y_prompt, y_sample, nk, nv)
```

```python
import numpy as np
import ml_dtypes
import concourse.bass as bass
import concourse.mybir as mybir
from concourse.bass_utils import run_bass_kernel_spmd

F32 = mybir.dt.float32
BF16 = mybir.dt.bfloat16
AF = mybir.ActivationFunctionType
ALU = mybir.AluOpType
AX = mybir.AxisListType

NCORES = 8
D = 2048
NCH = 16
DFF = 5632
NF = 44
EPS = 1e-6
SCALE = 128 ** -0.5
NEG = -1e30
KB = 1024
ARENA_BASE = 16512


class Op:
    __slots__ = ("eng", "kind", "fn", "deps", "signal", "ev", "waits", "key", "idx", "small", "force")


class View:
    __slots__ = ("ap", "space", "ranges")

    def __init__(self, ap, space, ranges):
        self.ap = ap
        self.space = space
        self.ranges = ranges

    def w(self, fn):
        return View(fn(self.ap), self.space, self.ranges)


def _merge(rs):
    rs = sorted(rs)
    out = []
    for lo, hi in rs:
        if out and lo <= out[-1][1]:
            out[-1][1] = max(out[-1][1], hi)
        else:
            out.append([lo, hi])
    return [(a, b) for a, b in out]


class Tens:
    def __init__(self, prog, name, shape, dtype, off=None, bank=None):
        self.shape = list(shape)
        self.es = 2 if dtype == BF16 else 4
        nc = prog.nc
        if bank is None:
            self.space = "s"
            self.off = off
            self.t = nc.alloc_sbuf_tensor_at(name, self.shape, dtype, offset=ARENA_BASE + off)
            n = 1
            for s in self.shape[1:]:
                n *= s
            assert off + n * self.es <= prog.arena_limit, (name, off, n * self.es)
        else:
            self.space = "p"
            self.off = bank * 2048
            self.t = nc.alloc_psum_tensor(name, self.shape, dtype)

    def __getitem__(self, key):
        if not isinstance(key, tuple):
            key = (key,)
        ap = self.t[key]
        if self.space == "p":
            return View(ap, "p", [(self.off, self.off + 2048)])
        fs = self.shape[1:]
        k = list(key[1:]) + [slice(None)] * (len(fs) - (len(key) - 1))
        idx = []
        for kk, s in zip(k, fs):
            if isinstance(kk, int):
                idx.append((kk, kk + 1))
            else:
                lo, hi, st = kk.indices(s)
                assert st == 1
                idx.append((lo, hi))
        strides = [1] * len(fs)
        for i in range(len(fs) - 2, -1, -1):
            strides[i] = strides[i + 1] * fs[i + 1]
        rs = [(0, 0)]
        starts = [0]
        for (lo, hi), st in zip(idx[:-1], strides[:-1]):
            starts = [s + j * st for s in starts for j in range(lo, hi)]
        lo, hi = idx[-1]
        rs = [(self.off + (s + lo) * self.es, self.off + (s + hi) * self.es) for s in starts]
        return View(ap, "s", _merge(rs))


class Prog:
    ENGS = ("pe", "act", "dve", "pool", "sp")

    def __init__(self, nc, arena_limit):
        self.nc = nc
        self.arena_limit = arena_limit
        self.ops = []
        self.recs = {"s": [], "p": []}
        self.nbank = 0

    def _record(self, op, reads, writes):
        deps = set()
        acc = []
        for v in reads:
            if isinstance(v, View):
                for r in v.ranges:
                    acc.append((v.space, r[0], r[1], False))
        op.small = False
        for v in writes:
            if isinstance(v, View):
                if v.space == "s" and sum(r[1] - r[0] for r in v.ranges) < 1024:
                    op.small = True
                for r in v.ranges:
                    acc.append((v.space, r[0], r[1], True))
        for sp, lo, hi, isw in acc:
            for rec in self.recs[sp]:
                if rec[0] < hi and lo < rec[1]:
                    if isw or rec[3]:
                        deps.add(rec[2])
                    elif sp == "p" and rec[2].eng != op.eng:
                        deps.add(rec[2])
        deps.discard(op)
        op.deps = deps
        for sp, lo, hi, isw in acc:
            recs = self.recs[sp]
            if isw:
                recs[:] = [r for r in recs if not (lo <= r[0] and r[1] <= hi)]
                recs.append((lo, hi, op, True))
            else:
                done = False
                for i, r in enumerate(recs):
                    if r[0] == lo and r[1] == hi and not r[3] and r[2].eng == op.eng and r[2].kind == "c" and op.kind == "c":
                        recs[i] = (lo, hi, op, False)
                        done = True
                        break
                if not done:
                    recs.append((lo, hi, op, False))

    def add(self, eng, fn, reads=(), writes=(), kind="c", key=None, extra=()):
        op = Op()
        op.eng, op.kind, op.fn, op.key = eng, kind, fn, key
        op.signal = False
        op.ev = None
        op.waits = []
        op.idx = len(self.ops)
        op.force = set()
        self._record(op, reads, writes)
        op.deps |= set(extra)
        if eng == "pe":
            if self.pe_fence and self.last_pe is not None:
                op.deps.add(self.last_pe)
                op.force.add(self.last_pe)
                self.pe_fence = False
            self.last_pe = op
        self.ops.append(op)
        return op

    rot = 8
    pe_fence = False
    last_pe = None

    def pe_barrier(self):
        self.pe_fence = True

    def next_bank(self):
        b = self.banks[self.nbank % self.rot]
        self.nbank += 1
        return b

    def emit(self):
        nc = self.nc
        for op in self.ops:
            for d in op.deps:
                if d.kind == "dma" or d.eng != op.eng or d.small or op.eng != "pe" or d in op.force:
                    d.signal = True
        esem = {e: nc.alloc_semaphore("sem_" + e) for e in self.ENGS}
        ecnt = {e: 0 for e in self.ENGS}
        dsem = {}
        dcnt = {}
        for op in self.ops:
            if op.kind == "dma":
                if op.key not in dsem:
                    dsem[op.key] = nc.alloc_semaphore("d_" + str(op.key))
                    dcnt[op.key] = 0
                dcnt[op.key] += 16
                op.ev = (dsem[op.key], dcnt[op.key])
            elif op.signal:
                ecnt[op.eng] += 1
                op.ev = (esem[op.eng], ecnt[op.eng])
        seen = {e: {} for e in self.ENGS}
        for op in self.ops:
            w = {}
            for d in op.deps:
                if d.kind != "dma" and d.eng == op.eng and not d.small and op.eng == "pe" and d not in op.force:
                    continue
                s, v = d.ev
                if seen[op.eng].get(s, 0) >= v:
                    continue
                if w.get(s, (None, 0))[1] < v:
                    w[s] = (s, v)
            for s, v in w.values():
                seen[op.eng][s] = v
            op.waits = list(w.values())
        per = {e: [o for o in self.ops if o.eng == e] for e in self.ENGS}

        def run(engobj, ops):
            for op in ops:
                for s, v in op.waits:
                    engobj.wait_ge(s, v)
                if op.fn is None:
                    continue
                ins = op.fn(engobj)
                if op.kind == "dma":
                    ins.then_inc(op.ev[0], 16)
                elif op.signal:
                    ins.then_inc(op.ev[0], 1)

        with nc.Block() as block:
            @block.sync
            def _(e):
                run(e, per["sp"])

            @block.gpsimd
            def _(e):
                run(e, per["pool"])

            @block.tensor
            def _(e):
                run(e, per["pe"])

            @block.vector
            def _(e):
                run(e, per["dve"])

            @block.scalar
            def _(e):
                run(e, per["act"])


def _ap(x):
    return x.ap if isinstance(x, View) else x


def build_program(debug=None):
    dumps = []
    nc = bass.Bass("TRN2", target_bir_lowering=False)
    limit = nc.sbuf_top - ARENA_BASE
    P = Prog(nc, limit)
    P.banks = [Tens(P, f"bank{i}", [128, 512], F32, bank=i) for i in range(8)]

    def din(name, shape, dt=F32):
        return nc.dram_tensor(name, list(shape), dt, kind="ExternalInput").ap()

    def dout(name, shape):
        return nc.dram_tensor(name, list(shape), F32, kind="ExternalOutput").ap()

    xp_d = din("xp", [512, D])
    xs_d = din("xs", [640, D])
    ck_d = din("ck", [512, 1024])
    cv_d = din("cv", [512, 1024])
    wada_d = din("w_ada", [D, 6 * D])
    win_d = din("w_in", [D, 4096])
    wout_d = din("w_out", [D, D])
    wg_d = din("w_gate", [D, DFF])
    wu_d = din("w_up", [D, DFF])
    wd_d = din("w_down", [DFF, D])
    wp_d = din("w_pool", [4, 256, 256])
    vecs_d = din("vecs", [128, 184])
    rows_d = din("rows", [128, 256])
    ident_d = din("ident", [128, 128])
    rpbg_d = din("rpbg", [8, 128, 4, 256])
    mask_d = din("maskneg", [128, 4, 256])
    As_d = din("poolA_s", [128, 5, 4, 256], BF16)
    Ap_d = din("poolA_p", [128, 2, 4, 256], BF16)
    invs_d = din("inv_s", [128, 4, 256])
    invp_d = din("inv_p", [128, 4, 256])
    yp_d = dout("yp", [512, D])
    ys_d = dout("ys", [256, D])
    nk_d = dout("nk", [512, 1024])
    nv_d = dout("nv", [512, 1024])

    def finish(extra_deps):
        stores = list(extra_deps)
        for name, view, shape, dt in dumps:
            dd = nc.dram_tensor(name, list(shape), dt, kind="ExternalOutput").ap()
            stores.append(dma("sp", dd, view, "dbg_" + name))
        P.add("sp", None, extra=stores)
        P.emit()
        return nc

    def dma(q, out, in_, key, reads=None, writes=None):
        r = [in_] if reads is None else reads
        w = [out] if writes is None else writes
        o, i = _ap(out), _ap(in_)
        return P.add(q, lambda e: e.dma_start(out=o, in_=i), reads=r, writes=w, kind="dma", key=key)

    def mm(out, lhsT, rhs, start, stop):
        o, l, r = _ap(out), _ap(lhsT), _ap(rhs)
        return P.add("pe", lambda e: e.matmul(o, lhsT=l, rhs=r, start=start, stop=stop), reads=[lhsT, rhs], writes=[out])

    def tr(out, in_, idv):
        o, i, d = _ap(out), _ap(in_), _ap(idv)
        return P.add("pe", lambda e: e.transpose(o, i, d), reads=[in_, idv], writes=[out])

    def act(out, in_, func, scale=1.0, bias=0.0, eng="act"):
        o, i, s, b = _ap(out), _ap(in_), _ap(scale), _ap(bias)
        return P.add("act", lambda e: e.activation(out=o, in_=i, func=func, bias=b, scale=s),
                     reads=[in_, scale, bias], writes=[out])

    def tt(out, in0, in1, op, eng="dve"):
        o, a, b = _ap(out), _ap(in0), _ap(in1)
        return P.add(eng, lambda e: e.tensor_tensor(out=o, in0=a, in1=b, op=op), reads=[in0, in1], writes=[out])

    def ts(out, in0, s1, s2, op0, op1=None, eng="dve"):
        o, a, x1, x2 = _ap(out), _ap(in0), _ap(s1), _ap(s2)
        if op1 is None:
            return P.add(eng, lambda e: e.tensor_scalar(out=o, in0=a, scalar1=x1, scalar2=None, op0=op0),
                         reads=[in0, s1], writes=[out])
        return P.add(eng, lambda e: e.tensor_scalar(out=o, in0=a, scalar1=x1, scalar2=x2, op0=op0, op1=op1),
                     reads=[in0, s1, s2], writes=[out])

    def stt(out, in0, scalar, in1, op0, op1, accum=None):
        o, a, s, b, ac = _ap(out), _ap(in0), _ap(scalar), _ap(in1), _ap(accum)
        if accum is None:
            return P.add("dve", lambda e: e.scalar_tensor_tensor(out=o, in0=a, scalar=s, in1=b, op0=op0, op1=op1),
                         reads=[in0, scalar, in1], writes=[out])
        return P.add("dve", lambda e: e.scalar_tensor_tensor(out=o, in0=a, scalar=s, in1=b, op0=op0, op1=op1,
                                                             accum_out=ac),
                     reads=[in0, scalar, in1], writes=[out, accum])

    def cp(out, in_, eng="dve"):
        o, i = _ap(out), _ap(in_)
        return P.add(eng, lambda e: e.tensor_copy(out=o, in_=i), reads=[in_], writes=[out])

    def recip(out, in_):
        o, i = _ap(out), _ap(in_)
        return P.add("dve", lambda e: e.reciprocal(out=o, in_=i), reads=[in_], writes=[out])

    def memset(out, val, eng="dve"):
        o = _ap(out)
        return P.add(eng, lambda e: e.memset(o, val), writes=[out])

    def flat(v):
        nd = len(v.ap.shape)
        if nd == 3:
            return v.w(lambda a: a.rearrange("p a b -> p (a b)"))
        return v

    def v3(v, a):
        return v.w(lambda x: x.rearrange("p (a b) -> p a b", a=a))

    def bc_last(v, shape):
        return v.w(lambda x: x.unsqueeze(2).to_broadcast(shape))

    def bc_mid(v, shape):
        return v.w(lambda x: x.unsqueeze(1).to_broadcast(shape))

    cpos = [0]

    def const(name, shape, dt):
        n = 1
        for s in shape[1:]:
            n *= s
        n *= 2 if dt == BF16 else 4
        off = cpos[0]
        cpos[0] = (off + n + 31) // 32 * 32
        assert cpos[0] <= 8 * KB
        return Tens(P, name, shape, dt, off=off)

    vecs = const("vecs", [128, 184], F32)
    rows = const("rows", [128, 256], F32)
    ident = const("ident", [128, 128], F32)
    onesf = const("onesf", [128, 128], F32)
    onesb = const("onesb", [128, 128], BF16)
    modT = const("modT", [128, 96, 2], F32)
    gsT = const("gsT", [128, 2, 16, 2], F32)
    sT = const("sT", [128, 16, 2], BF16)
    ssq = const("ssq", [128, 16], F32)
    rsq = const("rsq", [128, 16], F32)
    rstd = const("rstd", [128, 16], F32)
    ssq4 = const("ssq4", [128, 6, 4], F32)
    rsq4 = const("rsq4", [128, 6, 4], F32)
    rstd4 = const("rstd4", [128, 6, 4], F32)
    junkh = const("junkh", [128, 128], BF16)
    identb = const("identb", [128, 128], BF16)

    wada = [Tens(P, f"wada{i}", [128, 16, 512], BF16, off=(8 + 16 * i) * KB) for i in range(3)]
    hT = Tens(P, "hT", [128, 16, 1152], BF16, off=8 * KB)
    NQ = 6
    qst = [Tens(P, f"qst{i}", [128, 4, 128], F32, off=(44 + 2 * i) * KB) for i in range(NQ)]
    attn_g = Tens(P, "attn_g", [128, 8, 256], F32, off=8 * KB)
    pool_g = Tens(P, "pool_g", [128, 8, 256], F32, off=16 * KB)
    Ep = [Tens(P, f"Ep{i}", [128, 2, 256], BF16, off=(24 + i) * KB) for i in range(8)]
    Es = [Tens(P, f"Es{i}", [128, 8, 256], BF16, off=(24 + 4 * i) * KB) for i in range(2)]
    maskneg = Tens(P, "maskneg", [128, 4, 256], F32, off=32 * KB)
    rpbg = [Tens(P, f"rpbg{i}", [128, 4, 256], F32, off=(36 + 4 * i) * KB) for i in range(2)]
    bm = Tens(P, "bm", [128, 4, 256], F32, off=44 * KB)
    dT = Tens(P, "dT", [128, 8, 256], BF16, off=48 * KB)
    rden = [Tens(P, f"rden{i}", [128, 256], F32, off=(52 + i) * KB) for i in range(2)]
    rsn = Tens(P, "rsn", [128, 256], F32, off=54 * KB)
    rstdn = Tens(P, "rstdn", [128, 256], F32, off=55 * KB)
    x1 = Tens(P, "x1", [128, 6, D], F32, off=8 * KB)
    qT = Tens(P, "qT", [128, 8, 768], BF16, off=56 * KB)
    kT = Tens(P, "kT", [128, 8, 1024], BF16, off=68 * KB)
    V = Tens(P, "V", [128, 8, 1024], BF16, off=84 * KB)
    U = Tens(P, "U", [128, 9, 1024], BF16, off=100 * KB)
    ckT = Tens(P, "ckT", [128, 8, 512], BF16, off=118 * KB)
    CV = Tens(P, "CV", [128, 4, 1024], BF16, off=126 * KB)
    junk2 = Tens(P, "junk2", [128, D], BF16, off=104 * KB)
    diag2 = [Tens(P, f"diag2_{i}", [128, 128], F32, off=120 * KB + 512 * i) for i in range(6)]
    xa = [Tens(P, f"xa{i}", [128, D], F32, off=o * KB) for i, o in enumerate([56, 64, 72, 80, 88, 96, 104, 134, 142])]
    junkA = Tens(P, "junkA", [128, D], BF16, off=112 * KB)
    diagA = [Tens(P, f"diagA_{i}", [128, 128], F32, off=150 * KB + 512 * i) for i in range(8)]
    diagA.append(Tens(P, "diagA_8", [128, 128], F32, off=186 * KB))
    qb = [Tens(P, f"qb{i}", [128, 4, 128], BF16, off=(134 + i) * KB) for i in range(6)]
    attn_g2 = Tens(P, "attn_g2", [128, 8, 256], F32, off=32 * KB)
    pool_g2 = Tens(P, "pool_g2", [128, 8, 256], F32, off=40 * KB)
    aT = Tens(P, "aT", [128, NF, 768], BF16, off=56 * KB)
    woutr = [Tens(P, f"woutr{i}", [128, 16, 512], BF16, off=(56 + 16 * i) * KB) for i in range(2)]
    gbc1 = Tens(P, "gbc1", [128, 2, D], F32, off=88 * KB)
    wdr = [Tens(P, f"wdr{i}", [128, 4, 512], BF16, off=o * KB) for i, o in enumerate([158, 162, 166, 170, 174, 178])]
    scr = [Tens(P, f"scr{i}", [128, 512], F32, off=(124 + 2 * i) * KB) for i in range(4)]
    diag = [Tens(P, f"diag{i}", [128, 128], F32, off=132 * KB + 512 * i) for i in range(2)]
    winr = [Tens(P, f"winr{i}", [128, 16, 512], BF16, off=(154 + 16 * i) * KB) for i in range(2)]
    vst = [Tens(P, f"vst{i}", [128, 512], F32, off=(186 + 2 * i) * KB) for i in range(2)]
    NW2 = 4
    wada2 = [Tens(P, f"wada2_{i}", [128, 16, 128], BF16, off=(190 + 4 * i) * KB) for i in range(NW2)]
    ycatT = Tens(P, "ycatT", [128, 16, 768], BF16, off=134 * KB)
    sqbuf = Tens(P, "sqbuf", [128, 8, 256], F32, off=158 * KB)
    As = Tens(P, "As", [128, 5, 4, 256], BF16, off=166 * KB)
    Ap = Tens(P, "Ap", [128, 2, 4, 256], BF16, off=176 * KB)
    invs = Tens(P, "invs", [128, 4, 256], F32, off=176 * KB)
    invp = Tens(P, "invp", [128, 4, 256], F32, off=184 * KB)
    wpool = Tens(P, "wpool", [128, 4, 2, 256], BF16, off=180 * KB)
    tmpS = Tens(P, "tmpS", [128, 512], F32, off=184 * KB)
    h2T = Tens(P, "h2T", [128, 16, 768], BF16, off=134 * KB)
    gbc2 = Tens(P, "gbc2", [128, 2, D], F32, off=134 * KB)
    NFR = 4
    ffr = [(Tens(P, f"wg{i}", [128, 16, 128], BF16, off=(158 + 8 * i) * KB),
            Tens(P, f"wu{i}", [128, 16, 128], BF16, off=(162 + 8 * i) * KB)) for i in range(NFR)]

    def vcol(lo, hi):
        return vecs[:, lo:hi]

    C_COND, C_BADA, C_N1G, C_N2G, C_PSC, C_OAG, C_OPG = 0, 32, 128, 144, 160, 168, 176
    wada_v = wada_d.rearrange("(c p) e -> p c e", p=128)
    win_v = win_d.rearrange("(c p) e -> p c e", p=128)

    dma("sp", vecs[:, :], vecs_d[:, :], "c_vecs")
    dma("sp", rows[:, :], rows_d[:, :], "c_rows")
    dma("sp", ident[:, :], ident_d[:, :], "c_ident")
    memset(onesf[:, :], 1.0)
    cp(identb[:, :], ident[:, :])
    memset(onesb[:, :], 1.0)
    act(sT[:, :, :], v3(vcol(C_COND, C_COND + 32), 16), AF.Silu)

    NPRE_A = 3
    for blk in range(NPRE_A):
        dma("pool", wada[blk % 3][:, :, :], wada_v[:, :, blk * 512:(blk + 1) * 512], f"wada{blk % 3}")
    dma("pool", CV[:, :, :], cv_d.rearrange("(t p) e -> p t e", p=128), "c_cv")
    for t in range(7):
        src = xp_d[t * 128:(t + 1) * 128, :] if t < 4 else xs_d[(t - 4) * 128:(t - 3) * 128, :]
        dma("sp", xa[t][:, :], src, f"xa{t}")
    for t in range(4):
        st_ = xa[7 + t % 2]
        dma("sp", st_[:, 0:1024], ck_d[t * 128:(t + 1) * 128, :], f"xa{7 + t % 2}")
        for hb in range(2):
            bk = P.next_bank()
            for hh in range(4):
                h = hb * 4 + hh
                tr(bk[:, hh * 128:(hh + 1) * 128], st_[:, h * 128:(h + 1) * 128], ident[:, :])
            act(ckT[:, hb * 4:(hb + 1) * 4, t * 128:(t + 1) * 128], v3(bk[:, :], 4), AF.Copy)

    WIN_ORDER = [2, 3, 4, 5, 6, 7, 0, 1]

    def load_win(i):
        b = WIN_ORDER[i]
        dma("pool", winr[i % 2][:, :, :], win_v[:, :, b * 512:(b + 1) * 512], f"winr{i % 2}")

    bmod = P.banks[7]
    P.rot = 7

    def mod_evac(a, b):
        tt(modT[:, a:b, :], v3(bmod[:, 2 * a:2 * b], b - a), bc_last(vcol(C_BADA + a, C_BADA + b), [128, b - a, 2]),
           ALU.add)

    def make_gs(n, sc_lo, g_lo):
        ts(gsT[:, n, :, :], modT[:, sc_lo:sc_lo + 16, :], 1.0, None, ALU.add)
        tt(gsT[:, n, :, :], gsT[:, n, :, :], bc_last(vcol(g_lo, g_lo + 16), [128, 16, 2]), ALU.mult)

    for blk in range(8):
        wsl = wada[blk % 3]
        for jj in range(4):
            j = blk * 4 + jj
            for c in range(NCH):
                mm(bmod[:, j * 2:(j + 1) * 2], wsl[:, c, jj * 128:(jj + 1) * 128], sT[:, c, :], c == 0, c == NCH - 1)
        if blk + NPRE_A < 8:
            nb = blk + NPRE_A
            dma("pool", wada[nb % 3][:, :, :], wada_v[:, :, nb * 512:(nb + 1) * 512], f"wada{nb % 3}")
    P.nbank = 0
    mod_evac(0, 32)
    make_gs(0, 16, C_N1G)
    load_win(0)
    load_win(1)

    def ada2_gen():
        def ld(j):
            dma("pool", wada2[j % NW2][:, :, :], wada_v[:, :, j * 128:(j + 1) * 128], f"wada2_{j % NW2}")
        for j in range(32, 32 + NW2):
            ld(j)
        for j in range(32, 96):
            wsl = wada2[j % NW2]
            for c in range(NCH):
                mm(bmod[:, j * 2:(j + 1) * 2], wsl[:, c, :], sT[:, c, :], c == 0, c == NCH - 1)
            if j + NW2 < 96:
                ld(j + NW2)
            if j == 47:
                mod_evac(32, 48)
            elif j == 79:
                mod_evac(48, 80)
                make_gs(1, 64, C_N2G)
            elif j == 95:
                mod_evac(80, 96)
                P.rot = 8
            yield

    ada2 = ada2_gen()

    def ada2_step(n=1):
        for _ in range(n):
            next(ada2, None)

    def cond_of(t):
        return 0 if t < 4 else 1

    def rstd_chain(ssq_v, rsq_v, rstd_v, n):
        ts(rsq_v, ssq_v, 1.0 / n, EPS, ALU.mult, ALU.add)
        act(rsq_v, rsq_v, AF.Ln)
        act(rstd_v, rsq_v, AF.Exp, scale=-0.5)

    def norm_stats(src_view, junk, col, dg):
        P.add("act", (lambda e, o=_ap(junk[:, :]), i=_ap(src_view), a=_ap(ssq[:, col:col + 1]):
                      e.activation(out=o, in_=i, func=AF.Square, accum_out=a)),
              reads=[src_view], writes=[junk[:, :], ssq[:, col:col + 1]])
        rstd_chain(ssq[:, col:col + 1], rsq[:, col:col + 1], rstd[:, col:col + 1], D)
        ts(dg[:, :], ident[:, :], rstd[:, col:col + 1], None, ALU.mult)

    evi = [0]
    nstep = [0]

    def norm_group(srcs, dgs, n_idx, sh_lo, dst, dcol, cnd):
        nt = len(srcs)
        for c in range(NCH):
            bk = P.next_bank()
            for k in range(nt):
                mm(bk[:, k * 128:(k + 1) * 128], srcs[k](c), dgs[k][:, :], True, True)
            o = dst[:, c, dcol:dcol + nt * 128]
            i = bk[:, 0:nt * 128]
            g_ = gsT[:, n_idx, c, cnd:cnd + 1]
            s_ = modT[:, sh_lo + c, cnd:cnd + 1]
            evi[0] += 1
            if evi[0] % 2 == 0:
                act(o, i, AF.Identity, scale=g_, bias=s_)
            else:
                ts(o, i, g_, s_, ALU.mult, ALU.add)
            if n_idx == 0 and evi[0] % 6 == 0:
                ada2_step()
                nstep[0] += 1

    for t in range(7):
        norm_stats(xa[t][:, :], junkA, t, diagA[t])
    for t in range(7, 9):
        dma("sp", xa[t][:, :], xs_d[(t - 4) * 128:(t - 3) * 128, :], f"xa{t}")
        norm_stats(xa[t][:, :], junkA, t, diagA[t])

    def xsrc(t):
        return lambda c: xa[t][:, c * 128:(c + 1) * 128]

    norm_group([xsrc(t) for t in range(4)], diagA[0:4], 0, 0, hT, 0, 0)
    norm_group([xsrc(t) for t in range(4, 8)], diagA[4:8], 0, 0, hT, 512, 1)
    norm_group([xsrc(8)], diagA[8:9], 0, 0, hT, 1024, 1)

    if debug == "A":
        dumps.append(("dbg_modT", flat(modT[:, :, :]), [128, 192], F32))
        dumps.append(("dbg_hT", flat(hT[:, :, :]), [128, 16 * 1152], BF16))
        dumps.append(("dbg_rstd", rstd[:, :], [128, 16], F32))
        dumps.append(("dbg_ssq", ssq[:, :], [128, 16], F32))
        dumps.append(("dbg_ckT", flat(ckT[:, :, :]), [128, 8 * 512], BF16))
        return finish([])
    kv_stores = []
    qi = [0]
    vi = [0]
    pending = []
    TR_DELAY = 3
    bgrp = [0]

    def flush_pending(keep):
        while len(pending) > keep:
            pending.pop(0)()

    for i in range(8):
        b = WIN_ORDER[i]
        wsl = winr[i % 2]
        kind = "qkvu"[b // 2]
        half = b % 2
        tiles = {"q": [0, 1, 2, 3, 4, 5], "k": list(range(8)), "v": list(range(8)), "u": list(range(9))}[kind]
        for t in tiles:
            bk = P.next_bank()
            for c in range(NCH):
                mm(bk[:, :], hT[:, c, t * 128:(t + 1) * 128], wsl[:, c, :], c == 0, c == NCH - 1)
            bgrp[0] += 1
            if nstep[0] < 36 and bgrp[0] % 2 == 0:
                ada2_step()
                nstep[0] += 1
            flush_pending(TR_DELAY)
            if kind in "qk":
                s = qi[0] % NQ
                qi[0] += 1
                st_ = qst[s]
                act(flat(st_[:, :, :]), bk[:, :], AF.Copy)
                for hh in range(4):
                    stt(junkh[:, :], st_[:, hh, :], 1.0, st_[:, hh, :], ALU.mult, ALU.mult,
                        accum=ssq4[:, s, hh:hh + 1])
                rstd_chain(ssq4[:, s, :], rsq4[:, s, :], rstd4[:, s, :], 128)
                tt(st_[:, :, :], st_[:, :, :], bc_last(rstd4[:, s, :], [128, 4, 128]), ALU.mult)
                goff = 0 if kind == "q" else 128
                qb_ = qb[s]
                if kind == "q":
                    tt(qb_[:, :, :], st_[:, :, :], bc_mid(rows[:, goff:goff + 128], [128, 4, 128]), ALU.mult)
                else:
                    tt(st_[:, :, :], st_[:, :, :], bc_mid(rows[:, goff:goff + 128], [128, 4, 128]), ALU.mult)
                    act(flat(qb_[:, :, :]), flat(st_[:, :, :]), AF.Copy)
                    if t < 4:
                        kv_stores.append(dma("sp", nk_d[t * 128:(t + 1) * 128, half * 512:(half + 1) * 512],
                                             flat(st_[:, :, :]), f"qst{s}"))

                def job(qb_=qb_, kind=kind, half=half, t=t):
                    bk2 = P.next_bank()
                    bkb = bk2[:, :].w(lambda a: a.bitcast(BF16))
                    for hh in range(4):
                        tr(bkb.w(lambda a, hh=hh: a[:, hh * 128:(hh + 1) * 128]), qb_[:, hh, :], identb[:, :])
                    dstT = qT if kind == "q" else kT
                    act(dstT[:, half * 4:(half + 1) * 4, t * 128:(t + 1) * 128],
                        bkb.w(lambda a: a[:, 0:512].rearrange("p (a b) -> p a b", a=4)), AF.Copy)
                pending.append(job)
            elif kind == "v":
                act(V[:, t, half * 512:(half + 1) * 512], bk[:, :], AF.Copy)
                if t < 4:
                    s = vi[0] % 2
                    vi[0] += 1
                    cp(vst[s][:, :], bk[:, :])
                    kv_stores.append(dma("sp", nv_d[t * 128:(t + 1) * 128, half * 512:(half + 1) * 512],
                                         vst[s][:, :], f"vst{s}"))
            else:
                act(U[:, t, half * 512:(half + 1) * 512], bk[:, :], AF.Copy)
        if i + 2 < 8:
            load_win(i + 2)
    flush_pending(0)

    if debug == "B":
        dumps.append(("dbg_qT", flat(qT[:, :, :]), [128, 8 * 768], BF16))
        dumps.append(("dbg_kT", flat(kT[:, :, :]), [128, 8 * 1024], BF16))
        dumps.append(("dbg_V", flat(V[:, :, :]), [128, 8 * 1024], BF16))
        dumps.append(("dbg_U", flat(U[:, :, :]), [128, 9 * 1024], BF16))
        return finish(kv_stores)
    dma("sp", As[:, :, :, :], As_d[:, :, :, :], "c_As")
    dma("sp", Ap[:, :, :, :], Ap_d[:, :, :, :], "c_Ap")
    dma("sp", invp[:, :, :], invp_d[:, :, :], "c_invp")
    dma("pool", wpool[:, :, :, :], wp_d.rearrange("g (k p) e -> p g k e", p=128), "c_wpool")

    epi = [0]
    rdi = [0]

    def attn_gen(G, attn_dst):
        q0 = G * 256
        pend = []
        LA = 3 if G < 2 else 1

        def front(h):
            if G < 2:
                E = Ep[epi[0] % 8]
                epi[0] += 1
                bS = P.next_bank()
                for i in range(2):
                    mm(bS[:, i * 256:(i + 1) * 256], kT[:, h, q0 + i * 128:q0 + (i + 1) * 128], qT[:, h, q0:q0 + 256],
                       True, True)
                act(flat(E[:, 0:2, :]), bS[:, :], AF.Exp, scale=SCALE)
                pv = [(V[:, 2 * G + i, h * 128:(h + 1) * 128], E[:, i, :]) for i in range(2)]
            else:
                E = Es[h % 2]
                r = rpbg[h % 2]
                dma("sp", r[:, :, :], rpbg_d[h], f"rpbg{h % 2}")
                tt(bm[:, :, :], r[:, :, :], maskneg[:, :, :], ALU.add, eng="pool")
                bL = [P.next_bank(), P.next_bank()]
                for i in range(4):
                    mm(bL[i // 2][:, (i % 2) * 256:(i % 2 + 1) * 256], kT[:, h, 512 + i * 128:512 + (i + 1) * 128],
                       qT[:, h, 512:768], True, True)
                for j in range(2):
                    stt(tmpS[:, :], bL[j][:, :], SCALE, flat(bm[:, 2 * j:2 * j + 2, :]), ALU.mult, ALU.add)
                    act(flat(E[:, 2 * j:2 * j + 2, :]), tmpS[:, :], AF.Exp)
                bC = [P.next_bank(), P.next_bank()]
                for i in range(4):
                    mm(bC[i // 2][:, (i % 2) * 256:(i % 2 + 1) * 256], ckT[:, h, i * 128:(i + 1) * 128],
                       qT[:, h, 512:768], True, True)
                for j in range(2):
                    act(flat(E[:, 4 + 2 * j:6 + 2 * j, :]), bC[j][:, :], AF.Exp, scale=SCALE)
                pv = [(V[:, 4 + i, h * 128:(h + 1) * 128], E[:, i, :]) for i in range(4)]
                pv += [(CV[:, i, h * 128:(h + 1) * 128], E[:, 4 + i, :]) for i in range(4)]
            return (h, pv)

        def back(h, pv):
            bO = P.next_bank()
            n = len(pv)
            for i, (l, r_) in enumerate(pv):
                mm(bO[:, 0:256], l, r_, i == 0, i == n - 1)
            for i, (l, r_) in enumerate(pv):
                mm(bO[:, 256:512], onesb[:, :], r_, i == 0, i == n - 1)
            rd = rden[rdi[0] % 2]
            rdi[0] += 1
            act(rd[:, :], bO[:, 256:512], AF.Ln)
            act(rd[:, :], rd[:, :], AF.Exp, scale=-1.0)
            tt(attn_dst[:, h, :], bO[:, 0:256], rd[:, :], ALU.mult)

        for h in range(8):
            pend.append(front(h))
            if len(pend) > LA:
                back(*pend.pop(0))
            yield
        while pend:
            back(*pend.pop(0))
            yield

    def pool_gen(G, pool_dst):
        if G < 2:
            utiles, A, inv = [2 * G, 2 * G + 1], Ap, invp
        else:
            utiles, A, inv = [4, 5, 6, 7, 8], As, invs
        for cc2 in range(4):
            bD = P.next_bank()
            for k in range(2):
                cc = cc2 * 2 + k
                g = cc // 2
                for i, ut in enumerate(utiles):
                    mm(bD[:, k * 256:(k + 1) * 256], U[:, ut, cc * 128:(cc + 1) * 128], A[:, i, g, :], i == 0,
                       i == len(utiles) - 1)
            g = cc2
            tt(dT[:, 2 * cc2:2 * cc2 + 2, :], v3(bD[:, :], 2), bc_mid(inv[:, g, :], [128, 2, 256]), ALU.mult)
            yield
        for g in range(4):
            bY = P.next_bank()
            for eh in range(2):
                for k in range(2):
                    mm(bY[:, eh * 256:(eh + 1) * 256], wpool[:, g, k, eh * 128:(eh + 1) * 128], dT[:, 2 * g + k, :],
                       k == 0, k == 1)
            for eh in range(2):
                ee = 2 * g + eh
                act(pool_dst[:, ee, :], bY[:, eh * 256:(eh + 1) * 256], AF.Identity,
                    scale=vcol(C_PSC + ee, C_PSC + ee + 1))
            yield

    def merge_gen(G, a_src, p_src):
        q0 = G * 256
        for br, (src, gcol) in enumerate([(a_src, C_OAG), (p_src, C_OPG)]):
            tt(flat(sqbuf[:, :, :]), flat(src[:, :, :]), flat(src[:, :, :]), ALU.mult)
            yield
            bN = P.next_bank()
            for h in range(8):
                mm(bN[:, 0:256], onesf[:, :], sqbuf[:, h, :], h == 0, h == 7)
            ts(rsn[:, :], bN[:, 0:256], 1.0 / 1024, EPS, ALU.mult, ALU.add)
            act(rsn[:, :], rsn[:, :], AF.Ln)
            act(rstdn[:, :], rsn[:, :], AF.Exp, scale=-0.5)
            yield
            for h in range(8):
                stt(ycatT[:, br * 8 + h, q0:q0 + 256], src[:, h, :], vcol(gcol + h, gcol + h + 1), rstdn[:, :],
                    ALU.mult, ALU.mult)
                if h % 4 == 3:
                    yield

    bufs = [(attn_g, pool_g), (attn_g2, pool_g2), (attn_g, pool_g)]

    rnd = [0]

    def run_gens(alive):
        while alive:
            for g_ in list(alive):
                try:
                    next(g_)
                except StopIteration:
                    alive.remove(g_)
            rnd[0] += 1
            if nstep[0] < 48 and rnd[0] % 2 == 0:
                ada2_step()
                nstep[0] += 1

    run_gens([attn_gen(0, attn_g), pool_gen(0, pool_g)])
    run_gens([attn_gen(1, attn_g2), pool_gen(1, pool_g2), merge_gen(0, attn_g, pool_g)])
    run_gens([merge_gen(1, attn_g2, pool_g2)])
    dma("sp", maskneg[:, :, :], mask_d[:, :, :], "c_mask")
    dma("sp", invs[:, :, :], invs_d[:, :, :], "c_invs")
    run_gens([attn_gen(2, attn_g), pool_gen(2, pool_g)])
    run_gens([merge_gen(2, attn_g, pool_g)])
    while nstep[0] < 48:
        ada2_step()
        nstep[0] += 1

    if debug == "C":
        dumps.append(("dbg_ycatT", flat(ycatT[:, :, :]), [128, 16 * 768], BF16))
        dumps.append(("dbg_attn_g", flat(attn_g[:, :, :]), [128, 8 * 256], F32))
        dumps.append(("dbg_pool_g", flat(pool_g[:, :, :]), [128, 8 * 256], F32))
        return finish(kv_stores)
    def gate_bcast(gb, g_lo):
        di = 0
        for cnd in range(2):
            for cb in range(4):
                bk = P.next_bank()
                for k in range(4):
                    c = cb * 4 + k
                    dg = diag[di % 2]
                    di += 1
                    ts(dg[:, :], ident[:, :], modT[:, g_lo + c, cnd:cnd + 1], None, ALU.mult)
                    mm(bk[:, k * 128:(k + 1) * 128], onesf[:, :], dg[:, :], True, True)
                act(gb[:, cnd, cb * 512:(cb + 1) * 512], bk[:, :], AF.Copy)

    wout_v = wout_d.rearrange("(c p) e -> p c e", p=128)
    wg_v = wg_d.rearrange("(c p) e -> p c e", p=128)
    wu_v = wu_d.rearrange("(c p) e -> p c e", p=128)
    wi = [0]

    def load_wout(db):
        s_ = wi[0] % 2
        wi[0] += 1
        dma("pool", woutr[s_][:, :, :], wout_v[:, :, db * 512:(db + 1) * 512], f"woutr{s_}")
        return woutr[s_]

    def load_ff(i):
        wg_t, wu_t = ffr[i % NFR]
        dma("pool", wg_t[:, :, :], wg_v[:, :, i * 128:(i + 1) * 128], f"wg{i % NFR}")
        dma("pool", wu_t[:, :, :], wu_v[:, :, i * 128:(i + 1) * 128], f"wu{i % NFR}")

    wq = [load_wout(0), load_wout(1)]
    gate_bcast(gbc1, 32)
    for t in range(6):
        src = xp_d[t * 128:(t + 1) * 128, :] if t < 4 else xs_d[(t - 4) * 128:(t - 3) * 128, :]
        dma("sp", x1[:, t, :], src, f"x1_{t}")
    si = [0]

    def x1src(t):
        return lambda c: x1[:, t, c * 128:(c + 1) * 128]

    for db in range(4):
        wsl = wq.pop(0)
        for t in range(6):
            bk = P.next_bank()
            for e in range(16):
                mm(bk[:, :], ycatT[:, e, t * 128:(t + 1) * 128], wsl[:, e, :], e == 0, e == 15)
            s = scr[si[0] % 4]
            si[0] += 1
            tt(s[:, :], bk[:, :], gbc1[:, cond_of(t), db * 512:(db + 1) * 512], ALU.mult)
            tt(x1[:, t, db * 512:(db + 1) * 512], x1[:, t, db * 512:(db + 1) * 512], s[:, :], ALU.add, eng="pool")
            if db == 3:
                norm_stats(x1[:, t, :], junk2, 9 + t, diag2[t])
        if db + 2 < 4:
            wq.append(load_wout(db + 2))
        if db == 1:
            for i in range(NFR):
                load_ff(i)
    norm_group([x1src(t) for t in range(4)], diag2[0:4], 1, 48, h2T, 0, 0)
    norm_group([x1src(t) for t in range(4, 6)], diag2[4:6], 1, 48, h2T, 512, 1)

    if debug == "C3":
        dumps.append(("dbg_x1", flat(x1[:, :, :]), [128, 6 * D], F32))
        dumps.append(("dbg_gbc1", flat(gbc1[:, :, :]), [128, 2 * D], F32))
        dumps.append(("dbg_modT", flat(modT[:, :, :]), [128, 192], F32))
        return finish(kv_stores)
    for f in range(NF):
        wg_t, wu_t = ffr[f % NFR]
        bA, bB, bC_, bD_ = P.next_bank(), P.next_bank(), P.next_bank(), P.next_bank()
        for (w_t, b1, b2) in ((wg_t, bA, bB), (wu_t, bC_, bD_)):
            for c in range(NCH):
                mm(b1[:, :], w_t[:, c, :], h2T[:, c, 0:512], c == 0, c == NCH - 1)
                mm(b2[:, 0:256], w_t[:, c, :], h2T[:, c, 512:768], c == 0, c == NCH - 1)
        s = scr[si[0] % 4]
        si[0] += 1
        act(s[:, :], bA[:, :], AF.Silu)
        tt(aT[:, f, 0:512], s[:, :], bC_[:, :], ALU.mult)
        s = scr[si[0] % 4]
        si[0] += 1
        act(s[:, 0:256], bB[:, 0:256], AF.Silu)
        tt(aT[:, f, 512:768], s[:, 0:256], bD_[:, 0:256], ALU.mult)
        if f + NFR < NF:
            load_ff(f + NFR)
        if f >= 2:
            ada2_step()
    assert P.rot == 8

    def load_wd(i):
        db, fg = divmod(i, 11)
        dma("pool", wdr[i % 6][:, :, :],
            wd_d.rearrange("(f p) d -> p f d", p=128)[:, fg * 4:(fg + 1) * 4, db * 512:(db + 1) * 512], f"wdr{i % 6}")

    for i in range(6):
        load_wd(i)
    gate_bcast(gbc2, 80)
    ystores = []
    for db in range(4):
        acc = [P.next_bank() for _ in range(6)]
        for fg in range(11):
            i = db * 11 + fg
            wsl = wdr[i % 6]
            for fi in range(4):
                f = fg * 4 + fi
                for t in range(6):
                    mm(acc[t][:, :], aT[:, f, t * 128:(t + 1) * 128], wsl[:, fi, :], f == 0, f == NF - 1)
            if i + 6 < 44:
                load_wd(i + 6)
        for t in range(6):
            s = scr[si[0] % 4]
            si[0] += 1
            tt(s[:, :], acc[t][:, :], gbc2[:, cond_of(t), db * 512:(db + 1) * 512], ALU.mult)
            tt(x1[:, t, db * 512:(db + 1) * 512], x1[:, t, db * 512:(db + 1) * 512], s[:, :], ALU.add, eng="pool")
            dst = yp_d[t * 128:(t + 1) * 128, db * 512:(db + 1) * 512] if t < 4 else \
                ys_d[(t - 4) * 128:(t - 3) * 128, db * 512:(db + 1) * 512]
            ystores.append(dma("sp", dst, x1[:, t, db * 512:(db + 1) * 512], "ystore"))
    return finish(ystores + kv_stores)


def _sample_slots(cb):
    own = np.array([64 * r + 16 * cb + j for r in range(16) for j in range(16)], np.int64)
    ss = int(np.clip(16 * cb - 8, 0, 32))
    hcols = [c for c in range(ss, ss + 32) if not (16 * cb <= c < 16 * cb + 16)]
    halo = np.array([64 * r + c for r in range(16) for c in hcols], np.int64)
    extra = np.full(128, -1, np.int64)
    for r in range(16):
        for m in range(8):
            if cb == 0:
                tt_ = 64 * r - 8 + m
            elif cb == 3:
                tt_ = 64 * r + 64 + m
            else:
                tt_ = -1
            if 0 <= tt_ < 1024:
                extra[r * 8 + m] = tt_
    return own, halo, extra


def _pool_tables(slot_tok, own_tok, L):
    ns, no = len(slot_tok), len(own_tok)
    lookup = {int(t): s for s, t in enumerate(slot_tok) if t >= 0}
    A = np.zeros((ns, 4, no), np.float32)
    inv = np.zeros((4, no), np.float32)
    for g, w in enumerate((2, 4, 8, 16)):
        for o, T in enumerate(own_tok):
            lo = max(int(T) - w // 2, 0)
            hi = min(int(T) + w // 2, L)
            cnt = hi - lo
            for t2 in range(lo, hi):
                A[lookup[t2], g, o] += 1.0
            A[lookup[int(T)], g, o] -= cnt
            inv[g, o] = 1.0 / cnt
    return A, inv


def _bias_tables(cb, own, halo):
    keys = np.concatenate([own, halo])
    kr, kc = keys // 64, keys % 64
    qr, qc = own // 64, own % 64
    rs = np.clip(qr - 4, 0, 8)
    cs = np.clip(qc - 8, 0, 48)
    valid = (kr[:, None] >= rs[None, :]) & (kr[:, None] < rs[None, :] + 8) & \
            (kc[:, None] >= cs[None, :]) & (kc[:, None] < cs[None, :] + 16)
    dr = np.clip(kr[:, None] - qr[None, :] + 7, 0, 14)
    dc = np.clip(kc[:, None] - qc[None, :] + 15, 0, 30)
    return valid, dr, dc


_NC_CACHE = {}


def kernel(x_prompt, x_sample, cache_k, cache_v, c, c_ctx, w_ada, b_ada, norm1_g, w_in, q_norm_g, k_norm_g, rpb,
           w_pool, pool_scale, out_norm_attn_g, out_norm_pool_g, w_out, norm2_g, w_gate, w_up, w_down):
    f32 = np.float32
    A = lambda a: np.ascontiguousarray(np.asarray(a, dtype=f32))
    x_prompt, x_sample, cache_k, cache_v = A(x_prompt), A(x_sample), A(cache_k), A(cache_v)
    c, c_ctx, b_ada = A(c), A(c_ctx), A(b_ada)
    shared = {
        "w_ada": A(w_ada)[0], "w_in": A(w_in)[0], "w_out": A(w_out)[0], "w_gate": A(w_gate)[0],
        "w_up": A(w_up)[0], "w_down": A(w_down)[0], "w_pool": A(w_pool)[0],
        "ident": np.eye(128, dtype=f32),
    }
    rows = np.ascontiguousarray(np.broadcast_to(
        np.concatenate([A(q_norm_g)[0], A(k_norm_g)[0]])[None, :], (128, 256)))
    shared["rows"] = rows

    def pl(v, n):
        return np.asarray(v, f32).reshape(n, 128).T

    tokp = np.arange(256)
    Ap_, invp_ = _pool_tables(tokp, tokp, 256)
    shared["poolA_p"] = np.ascontiguousarray(Ap_.reshape(2, 128, 4, 256).transpose(1, 0, 2, 3)).astype(ml_dtypes.bfloat16)
    shared["inv_p"] = np.ascontiguousarray(np.broadcast_to(invp_[None], (128, 4, 256)))
    rpb0 = A(rpb)[0]

    in_maps = []
    owns = []
    for i in range(NCORES):
        b, cb = i // 4, i % 4
        own, halo, extra = _sample_slots(cb)
        owns.append(own)
        slots = np.concatenate([own, halo, extra])
        xs = np.zeros((640, D), f32)
        ok = slots >= 0
        xs[ok] = x_sample[b][slots[ok]]
        As_, invs_ = _pool_tables(slots, own, 1024)
        valid, dr, dc = _bias_tables(cb, own, halo)
        rg = rpb0[:, dr, dc]
        rg = np.where(valid[None], rg, f32(0.0))
        mneg = np.where(valid, f32(0.0), f32(NEG)).astype(f32)
        vecs = np.concatenate([
            np.stack([pl(c_ctx, 16), pl(c[b], 16)], axis=2).reshape(128, 32),
            pl(b_ada[0], 96), pl(A(norm1_g)[0], 16), pl(A(norm2_g)[0], 16), pl(A(pool_scale)[0], 8),
            pl(A(out_norm_attn_g)[0], 8), pl(A(out_norm_pool_g)[0], 8)], axis=1)
        m = dict(shared)
        m.update({
            "xp": x_prompt[2 * i:2 * i + 2].reshape(512, D),
            "xs": xs,
            "ck": cache_k[b, 0].reshape(512, 1024),
            "cv": cache_v[b, 0].reshape(512, 1024),
            "vecs": np.ascontiguousarray(vecs.astype(f32)),
            "rpbg": np.ascontiguousarray(rg.reshape(8, 4, 128, 256).transpose(0, 2, 1, 3).astype(f32)),
            "maskneg": np.ascontiguousarray(mneg.reshape(4, 128, 256).transpose(1, 0, 2)),
            "poolA_s": np.ascontiguousarray(As_.reshape(5, 128, 4, 256).transpose(1, 0, 2, 3)).astype(ml_dtypes.bfloat16),
            "inv_s": np.ascontiguousarray(np.broadcast_to(invs_[None], (128, 4, 256))),
        })
        in_maps.append(m)

    import os
    dbg = os.environ.get("KDEBUG")
    if dbg:
        nc = build_program(debug=dbg)
        res = run_bass_kernel_spmd(nc, in_maps[:1], core_ids=[0])
        return res.results[0], in_maps[0]
    if "nc" not in _NC_CACHE:
        _NC_CACHE["nc"] = build_program()
    nc = _NC_CACHE["nc"]
    res = run_bass_kernel_spmd(nc, in_maps, core_ids=list(range(NCORES)))
    y_prompt = np.zeros((16, 256, D), f32)
    y_sample = np.zeros((2, 1024, D), f32)
    nk = np.zeros((16, 1, 256, 8, 128), f32)
    nv = np.zeros((16, 1, 256, 8, 128), f32)
    for i in range(NCORES):
        r = res.results[i]
        y_prompt[2 * i:2 * i + 2] = np.asarray(r["yp"]).reshape(2, 256, D)
        y_sample[i // 4][owns[i]] = np.asarray(r["ys"])
        nk[2 * i:2 * i + 2, 0] = np.asarray(r["nk"]).reshape(2, 256, 8, 128)
        nv[2 * i:2 * i + 2, 0] = np.asarray(r["nv"]).reshape(2, 256, 8, 128)
    return (y_prompt, y_sample, nk, nv)
```

```python
import numpy as np
import ml_dtypes
import concourse.bass as bass
import concourse.mybir as mybir
from concourse.bass_utils import run_bass_kernel_spmd

F32 = mybir.dt.float32
BF16 = mybir.dt.bfloat16
AF = mybir.ActivationFunctionType
ALU = mybir.AluOpType
AX = mybir.AxisListType

NCORES = 8
D = 2048
NCH = 16
DFF = 5632
NF = 44
EPS = 1e-6
SCALE = 128 ** -0.5
NEG = -1e30
KB = 1024
ARENA_BASE = 16512


class Op:
    __slots__ = ("eng", "kind", "fn", "deps", "signal", "ev", "waits", "key", "idx", "small", "force")


class View:
    __slots__ = ("ap", "space", "ranges")

    def __init__(self, ap, space, ranges):
        self.ap = ap
        self.space = space
        self.ranges = ranges

    def w(self, fn):
        return View(fn(self.ap), self.space, self.ranges)


def _merge(rs):
    rs = sorted(rs)
    out = []
    for lo, hi in rs:
        if out and lo <= out[-1][1]:
            out[-1][1] = max(out[-1][1], hi)
        else:
            out.append([lo, hi])
    return [(a, b) for a, b in out]


class Tens:
    def __init__(self, prog, name, shape, dtype, off=None, bank=None):
        self.shape = list(shape)
        self.es = 2 if dtype == BF16 else 4
        nc = prog.nc
        if bank is None:
            self.space = "s"
            self.off = off
            self.t = nc.alloc_sbuf_tensor_at(name, self.shape, dtype, offset=ARENA_BASE + off)
            n = 1
            for s in self.shape[1:]:
                n *= s
            assert off + n * self.es <= prog.arena_limit, (name, off, n * self.es)
        else:
            self.space = "p"
            self.off = bank * 2048
            self.t = nc.alloc_psum_tensor(name, self.shape, dtype)

    def __getitem__(self, key):
        if not isinstance(key, tuple):
            key = (key,)
        ap = self.t[key]
        if self.space == "p":
            return View(ap, "p", [(self.off, self.off + 2048)])
        fs = self.shape[1:]
        k = list(key[1:]) + [slice(None)] * (len(fs) - (len(key) - 1))
        idx = []
        for kk, s in zip(k, fs):
            if isinstance(kk, int):
                idx.append((kk, kk + 1))
            else:
                lo, hi, st = kk.indices(s)
                assert st == 1
                idx.append((lo, hi))
        strides = [1] * len(fs)
        for i in range(len(fs) - 2, -1, -1):
            strides[i] = strides[i + 1] * fs[i + 1]
        rs = [(0, 0)]
        starts = [0]
        for (lo, hi), st in zip(idx[:-1], strides[:-1]):
            starts = [s + j * st for s in starts for j in range(lo, hi)]
        lo, hi = idx[-1]
        rs = [(self.off + (s + lo) * self.es, self.off + (s + hi) * self.es) for s in starts]
        return View(ap, "s", _merge(rs))


class Prog:
    ENGS = ("pe", "act", "dve", "pool", "sp")

    def __init__(self, nc, arena_limit):
        self.nc = nc
        self.arena_limit = arena_limit
        self.ops = []
        self.recs = {"s": [], "p": []}
        self.nbank = 0

    def _record(self, op, reads, writes):
        deps = set()
        acc = []
        for v in reads:
            if isinstance(v, View):
                for r in v.ranges:
                    acc.append((v.space, r[0], r[1], False))
        op.small = False
        for v in writes:
            if isinstance(v, View):
                if v.space == "s" and sum(r[1] - r[0] for r in v.ranges) < 1024:
                    op.small = True
                for r in v.ranges:
                    acc.append((v.space, r[0], r[1], True))
        for sp, lo, hi, isw in acc:
            for rec in self.recs[sp]:
                if rec[0] < hi and lo < rec[1]:
                    if isw or rec[3]:
                        deps.add(rec[2])
                    elif sp == "p" and rec[2].eng != op.eng:
                        deps.add(rec[2])
        deps.discard(op)
        op.deps = deps
        for sp, lo, hi, isw in acc:
            recs = self.recs[sp]
            if isw:
                recs[:] = [r for r in recs if not (lo <= r[0] and r[1] <= hi)]
                recs.append((lo, hi, op, True))
            else:
                done = False
                for i, r in enumerate(recs):
                    if r[0] == lo and r[1] == hi and not r[3] and r[2].eng == op.eng and r[2].kind == "c" and op.kind == "c":
                        recs[i] = (lo, hi, op, False)
                        done = True
                        break
                if not done:
                    recs.append((lo, hi, op, False))

    def add(self, eng, fn, reads=(), writes=(), kind="c", key=None, extra=()):
        op = Op()
        op.eng, op.kind, op.fn, op.key = eng, kind, fn, key
        op.signal = False
        op.ev = None
        op.waits = []
        op.idx = len(self.ops)
        op.force = set()
        self._record(op, reads, writes)
        op.deps |= set(extra)
        if eng == "pe":
            if self.pe_fence and self.last_pe is not None:
                op.deps.add(self.last_pe)
                op.force.add(self.last_pe)
                self.pe_fence = False
            self.last_pe = op
        self.ops.append(op)
        return op

    rot = 8
    pe_fence = False
    last_pe = None

    def pe_barrier(self):
        self.pe_fence = True

    def next_bank(self):
        b = self.banks[self.nbank % self.rot]
        self.nbank += 1
        return b

    def emit(self):
        nc = self.nc
        for op in self.ops:
            for d in op.deps:
                if d.kind == "dma" or d.eng != op.eng or d.small or op.eng != "pe" or d in op.force:
                    d.signal = True
        esem = {e: nc.alloc_semaphore("sem_" + e) for e in self.ENGS}
        ecnt = {e: 0 for e in self.ENGS}
        dsem = {}
        dcnt = {}
        for op in self.ops:
            if op.kind == "dma":
                if op.key not in dsem:
                    dsem[op.key] = nc.alloc_semaphore("d_" + str(op.key))
                    dcnt[op.key] = 0
                dcnt[op.key] += 16
                op.ev = (dsem[op.key], dcnt[op.key])
            elif op.signal:
                ecnt[op.eng] += 1
                op.ev = (esem[op.eng], ecnt[op.eng])
        seen = {e: {} for e in self.ENGS}
        for op in self.ops:
            w = {}
            for d in op.deps:
                if d.kind != "dma" and d.eng == op.eng and not d.small and op.eng == "pe" and d not in op.force:
                    continue
                s, v = d.ev
                if seen[op.eng].get(s, 0) >= v:
                    continue
                if w.get(s, (None, 0))[1] < v:
                    w[s] = (s, v)
            for s, v in w.values():
                seen[op.eng][s] = v
            op.waits = list(w.values())
        per = {e: [o for o in self.ops if o.eng == e] for e in self.ENGS}

        def run(engobj, ops):
            for op in ops:
                for s, v in op.waits:
                    engobj.wait_ge(s, v)
                if op.fn is None:
                    continue
                ins = op.fn(engobj)
                if op.kind == "dma":
                    ins.then_inc(op.ev[0], 16)
                elif op.signal:
                    ins.then_inc(op.ev[0], 1)

        with nc.Block() as block:
            @block.sync
            def _(e):
                run(e, per["sp"])

            @block.gpsimd
            def _(e):
                run(e, per["pool"])

            @block.tensor
            def _(e):
                run(e, per["pe"])

            @block.vector
            def _(e):
                run(e, per["dve"])

            @block.scalar
            def _(e):
                run(e, per["act"])


def _ap(x):
    return x.ap if isinstance(x, View) else x


def build_program(debug=None):
    dumps = []
    nc = bass.Bass("TRN2", target_bir_lowering=False)
    limit = nc.sbuf_top - ARENA_BASE
    P = Prog(nc, limit)
    P.banks = [Tens(P, f"bank{i}", [128, 512], F32, bank=i) for i in range(8)]

    def din(name, shape, dt=F32):
        return nc.dram_tensor(name, list(shape), dt, kind="ExternalInput").ap()

    def dout(name, shape):
        return nc.dram_tensor(name, list(shape), F32, kind="ExternalOutput").ap()

    xp_d = din("xp", [512, D])
    xs_d = din("xs", [640, D])
    ck_d = din("ck", [512, 1024])
    cv_d = din("cv", [512, 1024])
    wada_d = din("w_ada", [D, 6 * D])
    win_d = din("w_in", [D, 4096])
    wout_d = din("w_out", [D, D])
    wg_d = din("w_gate", [D, DFF])
    wu_d = din("w_up", [D, DFF])
    wd_d = din("w_down", [DFF, D])
    wp_d = din("w_pool", [4, 256, 256])
    vecs_d = din("vecs", [128, 184])
    rows_d = din("rows", [128, 256])
    ident_d = din("ident", [128, 128])
    rpbg_d = din("rpbg", [8, 128, 4, 256])
    mask_d = din("maskneg", [128, 4, 256])
    As_d = din("poolA_s", [128, 5, 4, 256], BF16)
    Ap_d = din("poolA_p", [128, 2, 4, 256], BF16)
    invs_d = din("inv_s", [128, 4, 256])
    invp_d = din("inv_p", [128, 4, 256])
    yp_d = dout("yp", [512, D])
    ys_d = dout("ys", [256, D])
    nk_d = dout("nk", [512, 1024])
    nv_d = dout("nv", [512, 1024])

    def finish(extra_deps):
        stores = list(extra_deps)
        for name, view, shape, dt in dumps:
            dd = nc.dram_tensor(name, list(shape), dt, kind="ExternalOutput").ap()
            stores.append(dma("sp", dd, view, "dbg_" + name))
        P.add("sp", None, extra=stores)
        P.emit()
        return nc

    def dma(q, out, in_, key, reads=None, writes=None):
        r = [in_] if reads is None else reads
        w = [out] if writes is None else writes
        o, i = _ap(out), _ap(in_)
        return P.add(q, lambda e: e.dma_start(out=o, in_=i), reads=r, writes=w, kind="dma", key=key)

    def mm(out, lhsT, rhs, start, stop):
        o, l, r = _ap(out), _ap(lhsT), _ap(rhs)
        return P.add("pe", lambda e: e.matmul(o, lhsT=l, rhs=r, start=start, stop=stop), reads=[lhsT, rhs], writes=[out])

    def tr(out, in_, idv):
        o, i, d = _ap(out), _ap(in_), _ap(idv)
        return P.add("pe", lambda e: e.transpose(o, i, d), reads=[in_, idv], writes=[out])

    def act(out, in_, func, scale=1.0, bias=0.0, eng="act"):
        o, i, s, b = _ap(out), _ap(in_), _ap(scale), _ap(bias)
        return P.add("act", lambda e: e.activation(out=o, in_=i, func=func, bias=b, scale=s),
                     reads=[in_, scale, bias], writes=[out])

    def tt(out, in0, in1, op, eng="dve"):
        o, a, b = _ap(out), _ap(in0), _ap(in1)
        return P.add(eng, lambda e: e.tensor_tensor(out=o, in0=a, in1=b, op=op), reads=[in0, in1], writes=[out])

    def ts(out, in0, s1, s2, op0, op1=None, eng="dve"):
        o, a, x1, x2 = _ap(out), _ap(in0), _ap(s1), _ap(s2)
        if op1 is None:
            return P.add(eng, lambda e: e.tensor_scalar(out=o, in0=a, scalar1=x1, scalar2=None, op0=op0),
                         reads=[in0, s1], writes=[out])
        return P.add(eng, lambda e: e.tensor_scalar(out=o, in0=a, scalar1=x1, scalar2=x2, op0=op0, op1=op1),
                     reads=[in0, s1, s2], writes=[out])

    def stt(out, in0, scalar, in1, op0, op1, accum=None):
        o, a, s, b, ac = _ap(out), _ap(in0), _ap(scalar), _ap(in1), _ap(accum)
        if accum is None:
            return P.add("dve", lambda e: e.scalar_tensor_tensor(out=o, in0=a, scalar=s, in1=b, op0=op0, op1=op1),
                         reads=[in0, scalar, in1], writes=[out])
        return P.add("dve", lambda e: e.scalar_tensor_tensor(out=o, in0=a, scalar=s, in1=b, op0=op0, op1=op1,
                                                             accum_out=ac),
                     reads=[in0, scalar, in1], writes=[out, accum])

    def cp(out, in_, eng="dve"):
        o, i = _ap(out), _ap(in_)
        return P.add(eng, lambda e: e.tensor_copy(out=o, in_=i), reads=[in_], writes=[out])

    def recip(out, in_):
        o, i = _ap(out), _ap(in_)
        return P.add("dve", lambda e: e.reciprocal(out=o, in_=i), reads=[in_], writes=[out])

    def memset(out, val, eng="dve"):
        o = _ap(out)
        return P.add(eng, lambda e: e.memset(o, val), writes=[out])

    def flat(v):
        nd = len(v.ap.shape)
        if nd == 3:
            return v.w(lambda a: a.rearrange("p a b -> p (a b)"))
        return v

    def v3(v, a):
        return v.w(lambda x: x.rearrange("p (a b) -> p a b", a=a))

    def bc_last(v, shape):
        return v.w(lambda x: x.unsqueeze(2).to_broadcast(shape))

    def bc_mid(v, shape):
        return v.w(lambda x: x.unsqueeze(1).to_broadcast(shape))

    cpos = [0]

    def const(name, shape, dt):
        n = 1
        for s in shape[1:]:
            n *= s
        n *= 2 if dt == BF16 else 4
        off = cpos[0]
        cpos[0] = (off + n + 31) // 32 * 32
        assert cpos[0] <= 8 * KB
        return Tens(P, name, shape, dt, off=off)

    vecs = const("vecs", [128, 184], F32)
    rows = const("rows", [128, 256], F32)
    ident = const("ident", [128, 128], F32)
    onesf = const("onesf", [128, 128], F32)
    onesb = const("onesb", [128, 128], BF16)
    modT = const("modT", [128, 96, 2], F32)
    gsT = const("gsT", [128, 2, 16, 2], F32)
    sT = const("sT", [128, 16, 2], BF16)
    ssq = const("ssq", [128, 16], F32)
    rsq = const("rsq", [128, 16], F32)
    rstd = const("rstd", [128, 16], F32)
    ssq4 = const("ssq4", [128, 6, 4], F32)
    rsq4 = const("rsq4", [128, 6, 4], F32)
    rstd4 = const("rstd4", [128, 6, 4], F32)
    junkh = const("junkh", [128, 128], BF16)
    mrow = [const(f"mrow{i}", [128, 256], F32) for i in range(2)]

    wada = [Tens(P, f"wada{i}", [128, 16, 512], BF16, off=(8 + 16 * i) * KB) for i in range(3)]
    hT = Tens(P, "hT", [128, 16, 1152], BF16, off=8 * KB)
    NQ = 6
    qst = [Tens(P, f"qst{i}", [128, 4, 128], F32, off=(44 + 2 * i) * KB) for i in range(NQ)]
    attn_g = Tens(P, "attn_g", [128, 8, 256], F32, off=8 * KB)
    pool_g = Tens(P, "pool_g", [128, 8, 256], F32, off=16 * KB)
    Ep = [Tens(P, f"Ep{i}", [128, 2, 256], BF16, off=(24 + i) * KB) for i in range(8)]
    Es = [Tens(P, f"Es{i}", [128, 8, 256], BF16, off=(24 + 4 * i) * KB) for i in range(2)]
    maskneg = Tens(P, "maskneg", [128, 4, 256], F32, off=32 * KB)
    rpbg = [Tens(P, f"rpbg{i}", [128, 4, 256], F32, off=(36 + 4 * i) * KB) for i in range(2)]
    bm = Tens(P, "bm", [128, 4, 256], F32, off=44 * KB)
    dT = Tens(P, "dT", [128, 8, 256], BF16, off=48 * KB)
    rden = [Tens(P, f"rden{i}", [128, 256], F32, off=(52 + i) * KB) for i in range(2)]
    rsn = Tens(P, "rsn", [128, 256], F32, off=54 * KB)
    rstdn = Tens(P, "rstdn", [128, 256], F32, off=55 * KB)
    x1 = Tens(P, "x1", [128, 6, D], F32, off=8 * KB)
    qT = Tens(P, "qT", [128, 8, 768], BF16, off=56 * KB)
    kT = Tens(P, "kT", [128, 8, 1024], BF16, off=68 * KB)
    V = Tens(P, "V", [128, 8, 1024], BF16, off=84 * KB)
    U = Tens(P, "U", [128, 9, 1024], BF16, off=100 * KB)
    ckT = Tens(P, "ckT", [128, 8, 512], BF16, off=118 * KB)
    CV = Tens(P, "CV", [128, 4, 1024], BF16, off=126 * KB)
    junk2 = Tens(P, "junk2", [128, D], BF16, off=104 * KB)
    diag2 = [Tens(P, f"diag2_{i}", [128, 128], F32, off=120 * KB + 512 * i) for i in range(6)]
    xa = [Tens(P, f"xa{i}", [128, D], F32, off=o * KB) for i, o in enumerate([56, 64, 72, 80, 88, 96, 104, 134, 142])]
    junkA = Tens(P, "junkA", [128, D], BF16, off=112 * KB)
    diagA = [Tens(P, f"diagA_{i}", [128, 128], F32, off=150 * KB + 512 * i) for i in range(8)]
    diagA.append(Tens(P, "diagA_8", [128, 128], F32, off=186 * KB))
    attn_g2 = Tens(P, "attn_g2", [128, 8, 256], F32, off=32 * KB)
    pool_g2 = Tens(P, "pool_g2", [128, 8, 256], F32, off=40 * KB)
    aT = Tens(P, "aT", [128, NF, 768], BF16, off=56 * KB)
    woutr = [Tens(P, f"woutr{i}", [128, 16, 512], BF16, off=(56 + 16 * i) * KB) for i in range(2)]
    gbc1 = Tens(P, "gbc1", [128, 2, D], F32, off=88 * KB)
    wdr = [Tens(P, f"wdr{i}", [128, 4, 512], BF16, off=o * KB) for i, o in enumerate([158, 162, 166, 170, 174, 178])]
    scr = [Tens(P, f"scr{i}", [128, 512], F32, off=(124 + 2 * i) * KB) for i in range(4)]
    diag = [Tens(P, f"diag{i}", [128, 128], F32, off=132 * KB + 512 * i) for i in range(2)]
    winr = [Tens(P, f"winr{i}", [128, 16, 512], BF16, off=(154 + 16 * i) * KB) for i in range(2)]
    vst = [Tens(P, f"vst{i}", [128, 512], F32, off=(186 + 2 * i) * KB) for i in range(2)]
    NW2 = 4
    wada2 = [Tens(P, f"wada2_{i}", [128, 16, 128], BF16, off=(190 + 4 * i) * KB) for i in range(NW2)]
    ycatT = Tens(P, "ycatT", [128, 16, 768], BF16, off=134 * KB)
    sqbuf = Tens(P, "sqbuf", [128, 8, 256], F32, off=158 * KB)
    As = Tens(P, "As", [128, 5, 4, 256], BF16, off=166 * KB)
    Ap = Tens(P, "Ap", [128, 2, 4, 256], BF16, off=176 * KB)
    invs = Tens(P, "invs", [128, 4, 256], F32, off=176 * KB)
    invp = Tens(P, "invp", [128, 4, 256], F32, off=184 * KB)
    wpool = Tens(P, "wpool", [128, 4, 2, 256], BF16, off=180 * KB)
    tmpS = Tens(P, "tmpS", [128, 512], F32, off=184 * KB)
    h2T = Tens(P, "h2T", [128, 16, 768], BF16, off=134 * KB)
    gbc2 = Tens(P, "gbc2", [128, 2, D], F32, off=134 * KB)
    NFR = 4
    ffr = [(Tens(P, f"wg{i}", [128, 16, 128], BF16, off=(158 + 8 * i) * KB),
            Tens(P, f"wu{i}", [128, 16, 128], BF16, off=(162 + 8 * i) * KB)) for i in range(NFR)]

    def vcol(lo, hi):
        return vecs[:, lo:hi]

    C_COND, C_BADA, C_N1G, C_N2G, C_PSC, C_OAG, C_OPG = 0, 32, 128, 144, 160, 168, 176
    wada_v = wada_d.rearrange("(c p) e -> p c e", p=128)
    win_v = win_d.rearrange("(c p) e -> p c e", p=128)

    dma("sp", vecs[:, :], vecs_d[:, :], "c_vecs")
    dma("sp", rows[:, :], rows_d[:, :], "c_rows")
    dma("sp", ident[:, :], ident_d[:, :], "c_ident")
    memset(onesf[:, :], 1.0)
    memset(onesb[:, :], 1.0)
    act(sT[:, :, :], v3(vcol(C_COND, C_COND + 32), 16), AF.Silu)

    NPRE_A = 3
    for blk in range(NPRE_A):
        dma("pool", wada[blk % 3][:, :, :], wada_v[:, :, blk * 512:(blk + 1) * 512], f"wada{blk % 3}")
    dma("pool", CV[:, :, :], cv_d.rearrange("(t p) e -> p t e", p=128), "c_cv")
    for t in range(7):
        src = xp_d[t * 128:(t + 1) * 128, :] if t < 4 else xs_d[(t - 4) * 128:(t - 3) * 128, :]
        dma("sp", xa[t][:, :], src, f"xa{t}")
    for t in range(4):
        st_ = xa[7 + t % 2]
        dma("sp", st_[:, 0:1024], ck_d[t * 128:(t + 1) * 128, :], f"xa{7 + t % 2}")
        for hb in range(2):
            bk = P.next_bank()
            for hh in range(4):
                h = hb * 4 + hh
                tr(bk[:, hh * 128:(hh + 1) * 128], st_[:, h * 128:(h + 1) * 128], ident[:, :])
            act(ckT[:, hb * 4:(hb + 1) * 4, t * 128:(t + 1) * 128], v3(bk[:, :], 4), AF.Copy)

    WIN_ORDER = [2, 3, 4, 5, 6, 7, 0, 1]

    def load_win(i):
        b = WIN_ORDER[i]
        dma("pool", winr[i % 2][:, :, :], win_v[:, :, b * 512:(b + 1) * 512], f"winr{i % 2}")

    bmod = P.banks[7]
    P.rot = 7

    def mod_evac(a, b):
        tt(modT[:, a:b, :], v3(bmod[:, 2 * a:2 * b], b - a), bc_last(vcol(C_BADA + a, C_BADA + b), [128, b - a, 2]),
           ALU.add)

    def make_gs(n, sc_lo, g_lo):
        ts(gsT[:, n, :, :], modT[:, sc_lo:sc_lo + 16, :], 1.0, None, ALU.add)
        tt(gsT[:, n, :, :], gsT[:, n, :, :], bc_last(vcol(g_lo, g_lo + 16), [128, 16, 2]), ALU.mult)

    for blk in range(8):
        wsl = wada[blk % 3]
        for jj in range(4):
            j = blk * 4 + jj
            for c in range(NCH):
                mm(bmod[:, j * 2:(j + 1) * 2], wsl[:, c, jj * 128:(jj + 1) * 128], sT[:, c, :], c == 0, c == NCH - 1)
        if blk + NPRE_A < 8:
            nb = blk + NPRE_A
            dma("pool", wada[nb % 3][:, :, :], wada_v[:, :, nb * 512:(nb + 1) * 512], f"wada{nb % 3}")
    P.nbank = 0
    mod_evac(0, 32)
    make_gs(0, 16, C_N1G)
    load_win(0)
    load_win(1)

    def ada2_gen():
        def ld(j):
            dma("pool", wada2[j % NW2][:, :, :], wada_v[:, :, j * 128:(j + 1) * 128], f"wada2_{j % NW2}")
        for j in range(32, 32 + NW2):
            ld(j)
        for j in range(32, 96):
            wsl = wada2[j % NW2]
            for c in range(NCH):
                mm(bmod[:, j * 2:(j + 1) * 2], wsl[:, c, :], sT[:, c, :], c == 0, c == NCH - 1)
            if j + NW2 < 96:
                ld(j + NW2)
            if j == 47:
                mod_evac(32, 48)
            elif j == 79:
                mod_evac(48, 80)
                make_gs(1, 64, C_N2G)
            elif j == 95:
                mod_evac(80, 96)
                P.rot = 8
            yield

    ada2 = ada2_gen()

    def ada2_step(n=1):
        for _ in range(n):
            next(ada2, None)

    def cond_of(t):
        return 0 if t < 4 else 1

    def rstd_chain(ssq_v, rsq_v, rstd_v, n):
        ts(rsq_v, ssq_v, 1.0 / n, EPS, ALU.mult, ALU.add)
        act(rsq_v, rsq_v, AF.Ln)
        act(rstd_v, rsq_v, AF.Exp, scale=-0.5)

    def norm_stats(src_view, junk, col, dg):
        P.add("act", (lambda e, o=_ap(junk[:, :]), i=_ap(src_view), a=_ap(ssq[:, col:col + 1]):
                      e.activation(out=o, in_=i, func=AF.Square, accum_out=a)),
              reads=[src_view], writes=[junk[:, :], ssq[:, col:col + 1]])
        rstd_chain(ssq[:, col:col + 1], rsq[:, col:col + 1], rstd[:, col:col + 1], D)
        ts(dg[:, :], ident[:, :], rstd[:, col:col + 1], None, ALU.mult)

    evi = [0]
    nstep = [0]

    def norm_group(srcs, dgs, n_idx, sh_lo, dst, dcol, cnd):
        nt = len(srcs)
        for c in range(NCH):
            bk = P.next_bank()
            for k in range(nt):
                mm(bk[:, k * 128:(k + 1) * 128], srcs[k](c), dgs[k][:, :], True, True)
            o = dst[:, c, dcol:dcol + nt * 128]
            i = bk[:, 0:nt * 128]
            g_ = gsT[:, n_idx, c, cnd:cnd + 1]
            s_ = modT[:, sh_lo + c, cnd:cnd + 1]
            evi[0] += 1
            if evi[0] % 2 == 0:
                act(o, i, AF.Identity, scale=g_, bias=s_)
            else:
                ts(o, i, g_, s_, ALU.mult, ALU.add)
            if n_idx == 0 and evi[0] % 3 == 0:
                ada2_step()
                nstep[0] += 1

    for t in range(7):
        norm_stats(xa[t][:, :], junkA, t, diagA[t])
    for t in range(7, 9):
        dma("sp", xa[t][:, :], xs_d[(t - 4) * 128:(t - 3) * 128, :], f"xa{t}")
        norm_stats(xa[t][:, :], junkA, t, diagA[t])

    def xsrc(t):
        return lambda c: xa[t][:, c * 128:(c + 1) * 128]

    norm_group([xsrc(t) for t in range(4)], diagA[0:4], 0, 0, hT, 0, 0)
    norm_group([xsrc(t) for t in range(4, 8)], diagA[4:8], 0, 0, hT, 512, 1)
    norm_group([xsrc(8)], diagA[8:9], 0, 0, hT, 1024, 1)

    if debug == "A":
        dumps.append(("dbg_modT", flat(modT[:, :, :]), [128, 192], F32))
        dumps.append(("dbg_hT", flat(hT[:, :, :]), [128, 16 * 1152], BF16))
        dumps.append(("dbg_rstd", rstd[:, :], [128, 16], F32))
        dumps.append(("dbg_ssq", ssq[:, :], [128, 16], F32))
        dumps.append(("dbg_ckT", flat(ckT[:, :, :]), [128, 8 * 512], BF16))
        return finish([])
    kv_stores = []
    qi = [0]
    vi = [0]
    pending = []
    TR_DELAY = 3
    bgrp = [0]

    def flush_pending(keep):
        while len(pending) > keep:
            pending.pop(0)()

    for i in range(8):
        b = WIN_ORDER[i]
        wsl = winr[i % 2]
        kind = "qkvu"[b // 2]
        half = b % 2
        tiles = {"q": [0, 1, 2, 3, 4, 5], "k": list(range(8)), "v": list(range(8)), "u": list(range(9))}[kind]
        for t in tiles:
            bk = P.next_bank()
            for c in range(NCH):
                mm(bk[:, :], hT[:, c, t * 128:(t + 1) * 128], wsl[:, c, :], c == 0, c == NCH - 1)
            bgrp[0] += 1
            if nstep[0] < 36 and bgrp[0] % 2 == 0:
                ada2_step()
                nstep[0] += 1
            flush_pending(TR_DELAY)
            if kind in "qk":
                s = qi[0] % NQ
                qi[0] += 1
                st_ = qst[s]
                act(flat(st_[:, :, :]), bk[:, :], AF.Copy)
                for hh in range(4):
                    stt(junkh[:, :], st_[:, hh, :], 1.0, st_[:, hh, :], ALU.mult, ALU.mult,
                        accum=ssq4[:, s, hh:hh + 1])
                rstd_chain(ssq4[:, s, :], rsq4[:, s, :], rstd4[:, s, :], 128)
                tt(st_[:, :, :], st_[:, :, :], bc_last(rstd4[:, s, :], [128, 4, 128]), ALU.mult)
                goff = 0 if kind == "q" else 128
                tt(st_[:, :, :], st_[:, :, :], bc_mid(rows[:, goff:goff + 128], [128, 4, 128]), ALU.mult)
                if kind == "k" and t < 4:
                    kv_stores.append(dma("sp", nk_d[t * 128:(t + 1) * 128, half * 512:(half + 1) * 512],
                                         flat(st_[:, :, :]), f"qst{s}"))

                def job(st_=st_, kind=kind, half=half, t=t):
                    bk2 = P.next_bank()
                    for hh in range(4):
                        tr(bk2[:, hh * 128:(hh + 1) * 128], st_[:, hh, :], ident[:, :])
                    dstT = qT if kind == "q" else kT
                    act(dstT[:, half * 4:(half + 1) * 4, t * 128:(t + 1) * 128], v3(bk2[:, :], 4), AF.Copy)
                pending.append(job)
            elif kind == "v":
                act(V[:, t, half * 512:(half + 1) * 512], bk[:, :], AF.Copy)
                if t < 4:
                    s = vi[0] % 2
                    vi[0] += 1
                    cp(vst[s][:, :], bk[:, :])
                    kv_stores.append(dma("sp", nv_d[t * 128:(t + 1) * 128, half * 512:(half + 1) * 512],
                                         vst[s][:, :], f"vst{s}"))
            else:
                act(U[:, t, half * 512:(half + 1) * 512], bk[:, :], AF.Copy)
        if i + 2 < 8:
            load_win(i + 2)
    flush_pending(0)

    if debug == "B":
        dumps.append(("dbg_qT", flat(qT[:, :, :]), [128, 8 * 768], BF16))
        dumps.append(("dbg_kT", flat(kT[:, :, :]), [128, 8 * 1024], BF16))
        dumps.append(("dbg_V", flat(V[:, :, :]), [128, 8 * 1024], BF16))
        dumps.append(("dbg_U", flat(U[:, :, :]), [128, 9 * 1024], BF16))
        return finish(kv_stores)
    dma("sp", As[:, :, :, :], As_d[:, :, :, :], "c_As")
    dma("sp", Ap[:, :, :, :], Ap_d[:, :, :, :], "c_Ap")
    dma("sp", invp[:, :, :], invp_d[:, :, :], "c_invp")
    dma("pool", wpool[:, :, :, :], wp_d.rearrange("g (k p) e -> p g k e", p=128), "c_wpool")

    epi = [0]
    rdi = [0]

    def attn_gen(G, attn_dst):
        q0 = G * 256
        pend = []
        LA = 3 if G < 2 else 1

        def front(h):
            if G < 2:
                E = Ep[epi[0] % 8]
                epi[0] += 1
                bS = P.next_bank()
                for i in range(2):
                    mm(bS[:, i * 256:(i + 1) * 256], kT[:, h, q0 + i * 128:q0 + (i + 1) * 128], qT[:, h, q0:q0 + 256],
                       True, True)
                act(flat(E[:, 0:2, :]), bS[:, :], AF.Exp, scale=SCALE)
                pv = [(V[:, 2 * G + i, h * 128:(h + 1) * 128], E[:, i, :]) for i in range(2)]
            else:
                E = Es[h % 2]
                r = rpbg[h % 2]
                dma("sp", r[:, :, :], rpbg_d[h], f"rpbg{h % 2}")
                tt(bm[:, :, :], r[:, :, :], maskneg[:, :, :], ALU.add, eng="pool")
                bL = [P.next_bank(), P.next_bank()]
                for i in range(4):
                    mm(bL[i // 2][:, (i % 2) * 256:(i % 2 + 1) * 256], kT[:, h, 512 + i * 128:512 + (i + 1) * 128],
                       qT[:, h, 512:768], True, True)
                for j in range(2):
                    stt(tmpS[:, :], bL[j][:, :], SCALE, flat(bm[:, 2 * j:2 * j + 2, :]), ALU.mult, ALU.add)
                    act(flat(E[:, 2 * j:2 * j + 2, :]), tmpS[:, :], AF.Exp)
                bC = [P.next_bank(), P.next_bank()]
                for i in range(4):
                    mm(bC[i // 2][:, (i % 2) * 256:(i % 2 + 1) * 256], ckT[:, h, i * 128:(i + 1) * 128],
                       qT[:, h, 512:768], True, True)
                for j in range(2):
                    act(flat(E[:, 4 + 2 * j:6 + 2 * j, :]), bC[j][:, :], AF.Exp, scale=SCALE)
                pv = [(V[:, 4 + i, h * 128:(h + 1) * 128], E[:, i, :]) for i in range(4)]
                pv += [(CV[:, i, h * 128:(h + 1) * 128], E[:, 4 + i, :]) for i in range(4)]
            return (h, pv)

        def back(h, pv):
            bO = P.next_bank()
            n = len(pv)
            for i, (l, r_) in enumerate(pv):
                mm(bO[:, 0:256], l, r_, i == 0, i == n - 1)
            for i, (l, r_) in enumerate(pv):
                mm(bO[:, 256:512], onesb[:, :], r_, i == 0, i == n - 1)
            rd = rden[rdi[0] % 2]
            rdi[0] += 1
            act(rd[:, :], bO[:, 256:512], AF.Ln)
            act(rd[:, :], rd[:, :], AF.Exp, scale=-1.0)
            tt(attn_dst[:, h, :], bO[:, 0:256], rd[:, :], ALU.mult)

        for h in range(8):
            pend.append(front(h))
            if len(pend) > LA:
                back(*pend.pop(0))
            yield
        while pend:
            back(*pend.pop(0))
            yield

    def pool_gen(G, pool_dst):
        if G < 2:
            utiles, A, inv = [2 * G, 2 * G + 1], Ap, invp
        else:
            utiles, A, inv = [4, 5, 6, 7, 8], As, invs
        for cc2 in range(4):
            bD = P.next_bank()
            for k in range(2):
                cc = cc2 * 2 + k
                g = cc // 2
                for i, ut in enumerate(utiles):
                    mm(bD[:, k * 256:(k + 1) * 256], U[:, ut, cc * 128:(cc + 1) * 128], A[:, i, g, :], i == 0,
                       i == len(utiles) - 1)
            g = cc2
            tt(dT[:, 2 * cc2:2 * cc2 + 2, :], v3(bD[:, :], 2), bc_mid(inv[:, g, :], [128, 2, 256]), ALU.mult)
            yield
        for g in range(4):
            bY = P.next_bank()
            for eh in range(2):
                for k in range(2):
                    mm(bY[:, eh * 256:(eh + 1) * 256], wpool[:, g, k, eh * 128:(eh + 1) * 128], dT[:, 2 * g + k, :],
                       k == 0, k == 1)
            for eh in range(2):
                ee = 2 * g + eh
                act(pool_dst[:, ee, :], bY[:, eh * 256:(eh + 1) * 256], AF.Identity,
                    scale=vcol(C_PSC + ee, C_PSC + ee + 1))
            yield

    def merge_gen(G, a_src, p_src):
        q0 = G * 256
        for br, (src, gcol) in enumerate([(a_src, C_OAG), (p_src, C_OPG)]):
            tt(flat(sqbuf[:, :, :]), flat(src[:, :, :]), flat(src[:, :, :]), ALU.mult)
            yield
            bN = P.next_bank()
            for h in range(8):
                mm(bN[:, 0:256], onesf[:, :], sqbuf[:, h, :], h == 0, h == 7)
            ts(rsn[:, :], bN[:, 0:256], 1.0 / 1024, EPS, ALU.mult, ALU.add)
            act(rsn[:, :], rsn[:, :], AF.Ln)
            act(rstdn[:, :], rsn[:, :], AF.Exp, scale=-0.5)
            yield
            for h in range(8):
                stt(ycatT[:, br * 8 + h, q0:q0 + 256], src[:, h, :], vcol(gcol + h, gcol + h + 1), rstdn[:, :],
                    ALU.mult, ALU.mult)
                if h % 4 == 3:
                    yield

    bufs = [(attn_g, pool_g), (attn_g2, pool_g2), (attn_g, pool_g)]

    rnd = [0]

    def run_gens(alive):
        while alive:
            for g_ in list(alive):
                try:
                    next(g_)
                except StopIteration:
                    alive.remove(g_)
            rnd[0] += 1
            if nstep[0] < 48 and rnd[0] % 2 == 0:
                ada2_step()
                nstep[0] += 1

    run_gens([attn_gen(0, attn_g), pool_gen(0, pool_g)])
    run_gens([attn_gen(1, attn_g2), pool_gen(1, pool_g2), merge_gen(0, attn_g, pool_g)])
    run_gens([merge_gen(1, attn_g2, pool_g2)])
    dma("sp", maskneg[:, :, :], mask_d[:, :, :], "c_mask")
    dma("sp", invs[:, :, :], invs_d[:, :, :], "c_invs")
    run_gens([attn_gen(2, attn_g), pool_gen(2, pool_g)])
    run_gens([merge_gen(2, attn_g, pool_g)])
    while nstep[0] < 48:
        ada2_step()
        nstep[0] += 1

    if debug == "C":
        dumps.append(("dbg_ycatT", flat(ycatT[:, :, :]), [128, 16 * 768], BF16))
        dumps.append(("dbg_attn_g", flat(attn_g[:, :, :]), [128, 8 * 256], F32))
        dumps.append(("dbg_pool_g", flat(pool_g[:, :, :]), [128, 8 * 256], F32))
        return finish(kv_stores)
    def gate_bcast(gb, g_lo):
        di = 0
        for cnd in range(2):
            for cb in range(4):
                bk = P.next_bank()
                for k in range(4):
                    c = cb * 4 + k
                    dg = diag[di % 2]
                    di += 1
                    ts(dg[:, :], ident[:, :], modT[:, g_lo + c, cnd:cnd + 1], None, ALU.mult)
                    mm(bk[:, k * 128:(k + 1) * 128], onesf[:, :], dg[:, :], True, True)
                act(gb[:, cnd, cb * 512:(cb + 1) * 512], bk[:, :], AF.Copy)

    wout_v = wout_d.rearrange("(c p) e -> p c e", p=128)
    wg_v = wg_d.rearrange("(c p) e -> p c e", p=128)
    wu_v = wu_d.rearrange("(c p) e -> p c e", p=128)
    wi = [0]

    def load_wout(db):
        s_ = wi[0] % 2
        wi[0] += 1
        dma("pool", woutr[s_][:, :, :], wout_v[:, :, db * 512:(db + 1) * 512], f"woutr{s_}")
        return woutr[s_]

    def load_ff(i):
        wg_t, wu_t = ffr[i % NFR]
        dma("pool", wg_t[:, :, :], wg_v[:, :, i * 128:(i + 1) * 128], f"wg{i % NFR}")
        dma("pool", wu_t[:, :, :], wu_v[:, :, i * 128:(i + 1) * 128], f"wu{i % NFR}")

    wq = [load_wout(0), load_wout(1)]
    gate_bcast(gbc1, 32)
    for t in range(6):
        src = xp_d[t * 128:(t + 1) * 128, :] if t < 4 else xs_d[(t - 4) * 128:(t - 3) * 128, :]
        dma("sp", x1[:, t, :], src, f"x1_{t}")
    si = [0]

    def x1src(t):
        return lambda c: x1[:, t, c * 128:(c + 1) * 128]

    for db in range(4):
        wsl = wq.pop(0)
        for t in range(6):
            bk = P.next_bank()
            for e in range(16):
                mm(bk[:, :], ycatT[:, e, t * 128:(t + 1) * 128], wsl[:, e, :], e == 0, e == 15)
            s = scr[si[0] % 4]
            si[0] += 1
            tt(s[:, :], bk[:, :], gbc1[:, cond_of(t), db * 512:(db + 1) * 512], ALU.mult)
            tt(x1[:, t, db * 512:(db + 1) * 512], x1[:, t, db * 512:(db + 1) * 512], s[:, :], ALU.add, eng="pool")
            if db == 3:
                norm_stats(x1[:, t, :], junk2, 9 + t, diag2[t])
        if db + 2 < 4:
            wq.append(load_wout(db + 2))
        if db == 1:
            for i in range(NFR):
                load_ff(i)
    norm_group([x1src(t) for t in range(4)], diag2[0:4], 1, 48, h2T, 0, 0)
    norm_group([x1src(t) for t in range(4, 6)], diag2[4:6], 1, 48, h2T, 512, 1)

    if debug == "C3":
        dumps.append(("dbg_x1", flat(x1[:, :, :]), [128, 6 * D], F32))
        dumps.append(("dbg_gbc1", flat(gbc1[:, :, :]), [128, 2 * D], F32))
        dumps.append(("dbg_modT", flat(modT[:, :, :]), [128, 192], F32))
        return finish(kv_stores)
    for f in range(NF):
        wg_t, wu_t = ffr[f % NFR]
        bA, bB, bC_, bD_ = P.next_bank(), P.next_bank(), P.next_bank(), P.next_bank()
        for (w_t, b1, b2) in ((wg_t, bA, bB), (wu_t, bC_, bD_)):
            for c in range(NCH):
                mm(b1[:, :], w_t[:, c, :], h2T[:, c, 0:512], c == 0, c == NCH - 1)
                mm(b2[:, 0:256], w_t[:, c, :], h2T[:, c, 512:768], c == 0, c == NCH - 1)
        s = scr[si[0] % 4]
        si[0] += 1
        act(s[:, :], bA[:, :], AF.Silu)
        tt(aT[:, f, 0:512], s[:, :], bC_[:, :], ALU.mult)
        s = scr[si[0] % 4]
        si[0] += 1
        act(s[:, 0:256], bB[:, 0:256], AF.Silu)
        tt(aT[:, f, 512:768], s[:, 0:256], bD_[:, 0:256], ALU.mult)
        if f + NFR < NF:
            load_ff(f + NFR)
        if f >= 2:
            ada2_step()
    assert P.rot == 8

    def load_wd(i):
        db, fg = divmod(i, 11)
        dma("pool", wdr[i % 6][:, :, :],
            wd_d.rearrange("(f p) d -> p f d", p=128)[:, fg * 4:(fg + 1) * 4, db * 512:(db + 1) * 512], f"wdr{i % 6}")

    for i in range(6):
        load_wd(i)
    gate_bcast(gbc2, 80)
    ystores = []
    for db in range(4):
        acc = [P.next_bank() for _ in range(6)]
        for fg in range(11):
            i = db * 11 + fg
            wsl = wdr[i % 6]
            for fi in range(4):
                f = fg * 4 + fi
                for t in range(6):
                    mm(acc[t][:, :], aT[:, f, t * 128:(t + 1) * 128], wsl[:, fi, :], f == 0, f == NF - 1)
            if i + 6 < 44:
                load_wd(i + 6)
        for t in range(6):
            s = scr[si[0] % 4]
            si[0] += 1
            tt(s[:, :], acc[t][:, :], gbc2[:, cond_of(t), db * 512:(db + 1) * 512], ALU.mult)
            tt(x1[:, t, db * 512:(db + 1) * 512], x1[:, t, db * 512:(db + 1) * 512], s[:, :], ALU.add, eng="pool")
            dst = yp_d[t * 128:(t + 1) * 128, db * 512:(db + 1) * 512] if t < 4 else \
                ys_d[(t - 4) * 128:(t - 3) * 128, db * 512:(db + 1) * 512]
            ystores.append(dma("sp", dst, x1[:, t, db * 512:(db + 1) * 512], "ystore"))
    return finish(ystores + kv_stores)


def _sample_slots(cb):
    own = np.array([64 * r + 16 * cb + j for r in range(16) for j in range(16)], np.int64)
    ss = int(np.clip(16 * cb - 8, 0, 32))
    hcols = [c for c in range(ss, ss + 32) if not (16 * cb <= c < 16 * cb + 16)]
    halo = np.array([64 * r + c for r in range(16) for c in hcols], np.int64)
    extra = np.full(128, -1, np.int64)
    for r in range(16):
        for m in range(8):
            if cb == 0:
                tt_ = 64 * r - 8 + m
            elif cb == 3:
                tt_ = 64 * r + 64 + m
            else:
                tt_ = -1
            if 0 <= tt_ < 1024:
                extra[r * 8 + m] = tt_
    return own, halo, extra


def _pool_tables(slot_tok, own_tok, L):
    ns, no = len(slot_tok), len(own_tok)
    lookup = {int(t): s for s, t in enumerate(slot_tok) if t >= 0}
    A = np.zeros((ns, 4, no), np.float32)
    inv = np.zeros((4, no), np.float32)
    for g, w in enumerate((2, 4, 8, 16)):
        for o, T in enumerate(own_tok):
            lo = max(int(T) - w // 2, 0)
            hi = min(int(T) + w // 2, L)
            cnt = hi - lo
            for t2 in range(lo, hi):
                A[lookup[t2], g, o] += 1.0
            A[lookup[int(T)], g, o] -= cnt
            inv[g, o] = 1.0 / cnt
    return A, inv


def _bias_tables(cb, own, halo):
    keys = np.concatenate([own, halo])
    kr, kc = keys // 64, keys % 64
    qr, qc = own // 64, own % 64
    rs = np.clip(qr - 4, 0, 8)
    cs = np.clip(qc - 8, 0, 48)
    valid = (kr[:, None] >= rs[None, :]) & (kr[:, None] < rs[None, :] + 8) & \
            (kc[:, None] >= cs[None, :]) & (kc[:, None] < cs[None, :] + 16)
    dr = np.clip(kr[:, None] - qr[None, :] + 7, 0, 14)
    dc = np.clip(kc[:, None] - qc[None, :] + 15, 0, 30)
    return valid, dr, dc


_NC_CACHE = {}


def kernel(x_prompt, x_sample, cache_k, cache_v, c, c_ctx, w_ada, b_ada, norm1_g, w_in, q_norm_g, k_norm_g, rpb,
           w_pool, pool_scale, out_norm_attn_g, out_norm_pool_g, w_out, norm2_g, w_gate, w_up, w_down):
    f32 = np.float32
    A = lambda a: np.ascontiguousarray(np.asarray(a, dtype=f32))
    x_prompt, x_sample, cache_k, cache_v = A(x_prompt), A(x_sample), A(cache_k), A(cache_v)
    c, c_ctx, b_ada = A(c), A(c_ctx), A(b_ada)
    shared = {
        "w_ada": A(w_ada)[0], "w_in": A(w_in)[0], "w_out": A(w_out)[0], "w_gate": A(w_gate)[0],
        "w_up": A(w_up)[0], "w_down": A(w_down)[0], "w_pool": A(w_pool)[0],
        "ident": np.eye(128, dtype=f32),
    }
    rows = np.ascontiguousarray(np.broadcast_to(
        np.concatenate([A(q_norm_g)[0], A(k_norm_g)[0]])[None, :], (128, 256)))
    shared["rows"] = rows

    def pl(v, n):
        return np.asarray(v, f32).reshape(n, 128).T

    tokp = np.arange(256)
    Ap_, invp_ = _pool_tables(tokp, tokp, 256)
    shared["poolA_p"] = np.ascontiguousarray(Ap_.reshape(2, 128, 4, 256).transpose(1, 0, 2, 3)).astype(ml_dtypes.bfloat16)
    shared["inv_p"] = np.ascontiguousarray(np.broadcast_to(invp_[None], (128, 4, 256)))
    rpb0 = A(rpb)[0]

    in_maps = []
    owns = []
    for i in range(NCORES):
        b, cb = i // 4, i % 4
        own, halo, extra = _sample_slots(cb)
        owns.append(own)
        slots = np.concatenate([own, halo, extra])
        xs = np.zeros((640, D), f32)
        ok = slots >= 0
        xs[ok] = x_sample[b][slots[ok]]
        As_, invs_ = _pool_tables(slots, own, 1024)
        valid, dr, dc = _bias_tables(cb, own, halo)
        rg = rpb0[:, dr, dc]
        rg = np.where(valid[None], rg, f32(0.0))
        mneg = np.where(valid, f32(0.0), f32(NEG)).astype(f32)
        vecs = np.concatenate([
            np.stack([pl(c_ctx, 16), pl(c[b], 16)], axis=2).reshape(128, 32),
            pl(b_ada[0], 96), pl(A(norm1_g)[0], 16), pl(A(norm2_g)[0], 16), pl(A(pool_scale)[0], 8),
            pl(A(out_norm_attn_g)[0], 8), pl(A(out_norm_pool_g)[0], 8)], axis=1)
        m = dict(shared)
        m.update({
            "xp": x_prompt[2 * i:2 * i + 2].reshape(512, D),
            "xs": xs,
            "ck": cache_k[b, 0].reshape(512, 1024),
            "cv": cache_v[b, 0].reshape(512, 1024),
            "vecs": np.ascontiguousarray(vecs.astype(f32)),
            "rpbg": np.ascontiguousarray(rg.reshape(8, 4, 128, 256).transpose(0, 2, 1, 3).astype(f32)),
            "maskneg": np.ascontiguousarray(mneg.reshape(4, 128, 256).transpose(1, 0, 2)),
            "poolA_s": np.ascontiguousarray(As_.reshape(5, 128, 4, 256).transpose(1, 0, 2, 3)).astype(ml_dtypes.bfloat16),
            "inv_s": np.ascontiguousarray(np.broadcast_to(invs_[None], (128, 4, 256))),
        })
        in_maps.append(m)

    import os
    dbg = os.environ.get("KDEBUG")
    if dbg:
        nc = build_program(debug=dbg)
        res = run_bass_kernel_spmd(nc, in_maps[:1], core_ids=[0])
        return res.results[0], in_maps[0]
    if "nc" not in _NC_CACHE:
        _NC_CACHE["nc"] = build_program()
    nc = _NC_CACHE["nc"]
    res = run_bass_kernel_spmd(nc, in_maps, core_ids=list(range(NCORES)))
    y_prompt = np.zeros((16, 256, D), f32)
    y_sample = np.zeros((2, 1024, D), f32)
    nk = np.zeros((16, 1, 256, 8, 128), f32)
    nv = np.zeros((16, 1, 256, 8, 128), f32)
    for i in range(NCORES):
        r = res.results[i]
        y_prompt[2 * i:2 * i + 2] = np.asarray(r["yp"]).reshape(2, 256, D)
        y_sample[i // 4][owns[i]] = np.asarray(r["ys"])
        nk[2 * i:2 * i + 2, 0] = np.asarray(r["nk"]).reshape(2, 256, 8, 128)
        nv[2 * i:2 * i + 2, 0] = np.asarray(r["nv"]).reshape(2, 256, 8, 128)
    return (y_prompt, y_sample, nk, nv)
```
